# Optimizing a Trainium2 kernel written in Bass

```python
import jax, jax.numpy as jnp
from jax import lax
import numpy as np

D_MODEL = 4096
BATCH = 4
SEQ = 4096
DEPTH = 2

CHUNK = 64
N_META = 16
Q_BLOCK = 128
RMS_EPS = 1e-6

LRU_WIDTH = D_MODEL // 2
LRU_BLOCKS = 16
LRU_BLOCK_DIM = LRU_WIDTH // LRU_BLOCKS
CONV_WIDTH = 4
LRU_C = 8.0
FOX_HEADS = 16
FOX_HEAD_DIM = (D_MODEL // 2) // FOX_HEADS
FOX_WIDTH = FOX_HEADS * FOX_HEAD_DIM
AB_IN = 2 * LRU_WIDTH + 3 * FOX_WIDTH + FOX_HEADS
AB_MIX = LRU_WIDTH + FOX_WIDTH

RET_HEADS = 16
RET_QK_DIM = D_MODEL // RET_HEADS
RET_V_DIM = 2 * D_MODEL // RET_HEADS
RET_QK_WIDTH = RET_HEADS * RET_QK_DIM
RET_V_WIDTH = RET_HEADS * RET_V_DIM
RET_IN = 2 * RET_QK_WIDTH + 2 * RET_V_WIDTH
ROPE_BASE = 10000.0

D_FF = -(-8 * D_MODEL // (3 * 256)) * 256

N_EVEN = (DEPTH + 1) // 2
N_ODD = DEPTH // 2

kernel_name = "hybrid_rglru_fox_retention_trunk"


def rms_norm(x, g):
    xf = x.astype(jnp.float32)
    y = xf * lax.rsqrt(jnp.mean(xf * xf, axis=-1, keepdims=True) + RMS_EPS)
    return (y * g.astype(jnp.float32)).astype(x.dtype)


def swiglu(x, w_gate, w_up, w_down):
    return (jax.nn.silu(x @ w_gate) * (x @ w_up)) @ w_down


def causal_depthwise_conv(x, w, b):
    L = x.shape[1]
    K = w.shape[0]
    xp = jnp.pad(x, ((0, 0), (K - 1, 0), (0, 0)))
    y = b
    for j in range(K):
        y = y + xp[:, j:j + L] * w[j]
    return y


def rg_lru(x, w_a, b_a, w_x, b_x, lam):
    Bsz, L, C = x.shape
    xb = x.reshape(Bsz, L, LRU_BLOCKS, LRU_BLOCK_DIM)
    r = jax.nn.sigmoid(jnp.einsum('blnc,ncd->blnd', xb, w_a).reshape(Bsz, L, C) + b_a)
    i = jax.nn.sigmoid(jnp.einsum('blnc,ncd->blnd', xb, w_x).reshape(Bsz, L, C) + b_x)
    log_a = -LRU_C * jax.nn.softplus(-lam.astype(jnp.float32)) * r.astype(jnp.float32)
    a = jnp.exp(log_a)
    gated_x = jnp.sqrt(-jnp.expm1(2.0 * log_a)) * (i * x).astype(jnp.float32)

    def combine(p, q):
        a1, b1 = p
        a2, b2 = q
        return a1 * a2, a2 * b1 + b2

    _, h = lax.associative_scan(combine, (a, gated_x), axis=1)
    return h.astype(x.dtype)


def forgetting_attention(q, k, v, f_logit):
    Bsz, L, H, Dh = q.shape
    log_f = jax.nn.log_sigmoid(f_logit.astype(jnp.float32))
    cum = jnp.cumsum(log_f, axis=1).transpose(0, 2, 1)
    n_blocks = -(-L // Q_BLOCK)
    pad = n_blocks * Q_BLOCK - L
    qp = jnp.pad(q, ((0, 0), (0, pad), (0, 0), (0, 0)))
    cq = jnp.pad(cum, ((0, 0), (0, 0), (0, pad)))
    scale = Dh ** -0.5
    key_pos = jnp.arange(L)

    def block(i):
        start = i * Q_BLOCK
        qb = lax.dynamic_slice_in_dim(qp, start, Q_BLOCK, axis=1)
        cb = lax.dynamic_slice_in_dim(cq, start, Q_BLOCK, axis=2)
        s = jnp.einsum('bqhd,bkhd->bhqk', qb, k).astype(jnp.float32) * scale
        s = s + cb[..., None] - cum[:, :, None, :]
        qpos = start + jnp.arange(Q_BLOCK)
        s = jnp.where(key_pos[None, :] <= qpos[:, None], s, -jnp.inf)
        p = jax.nn.softmax(s, axis=-1).astype(v.dtype)
        return jnp.einsum('bhqk,bkhd->bqhd', p, v)

    out = lax.map(block, jnp.arange(n_blocks))
    out = out.transpose(1, 0, 2, 3, 4).reshape(Bsz, n_blocks * Q_BLOCK, H, Dh)
    return out[:, :L]


def lru_fox_mixer(h, w_in, b_f, conv_w, conv_b, w_a, b_a, w_x, b_x, lam, q_norm, k_norm, w_out):
    Bsz, L, _ = h.shape
    z = h @ w_in
    cuts = [LRU_WIDTH, 2 * LRU_WIDTH, 2 * LRU_WIDTH + FOX_WIDTH,
            2 * LRU_WIDTH + 2 * FOX_WIDTH, 2 * LRU_WIDTH + 3 * FOX_WIDTH]
    x_lru, gate, q, k, v, f = jnp.split(z, cuts, axis=-1)
    x_lru = causal_depthwise_conv(x_lru, conv_w, conv_b)
    y_lru = rg_lru(x_lru, w_a, b_a, w_x, b_x, lam) * jax.nn.gelu(gate)
    q = rms_norm(q.reshape(Bsz, L, FOX_HEADS, FOX_HEAD_DIM), q_norm)
    k = rms_norm(k.reshape(Bsz, L, FOX_HEADS, FOX_HEAD_DIM), k_norm)
    v = v.reshape(Bsz, L, FOX_HEADS, FOX_HEAD_DIM)
    y_fox = forgetting_attention(q, k, v, f + b_f).reshape(Bsz, L, FOX_WIDTH)
    return jnp.concatenate([y_lru, y_fox.astype(y_lru.dtype)], axis=-1) @ w_out


def rotary(x, pos):
    half = x.shape[-1] // 2
    inv = ROPE_BASE ** (-jnp.arange(half, dtype=jnp.float32) / half)
    ang = pos[:, None].astype(jnp.float32) * inv[None, :]
    cos = jnp.cos(ang)[None, :, None, :]
    sin = jnp.sin(ang)[None, :, None, :]
    xf = x.astype(jnp.float32)
    x1, x2 = xf[..., :half], xf[..., half:]
    return jnp.concatenate([x1 * cos - x2 * sin, x1 * sin + x2 * cos], axis=-1).astype(x.dtype)


def retention_mixer(h, w_in, ret_norm, w_out):
    Bsz, L, _ = h.shape
    z = h @ w_in
    q, k, v, g = jnp.split(z, [RET_QK_WIDTH, 2 * RET_QK_WIDTH, 2 * RET_QK_WIDTH + RET_V_WIDTH], axis=-1)
    pos = jnp.arange(L)
    q = rotary(q.reshape(Bsz, L, RET_HEADS, RET_QK_DIM), pos)
    k = rotary(k.reshape(Bsz, L, RET_HEADS, RET_QK_DIM), pos) * (RET_QK_DIM ** -0.5)
    v = v.reshape(Bsz, L, RET_HEADS, RET_V_DIM)
    lead = (-N_META) % CHUNK
    padw = ((0, 0), (lead, 0), (0, 0), (0, 0))
    Lp = L + lead
    n_chunks = Lp // CHUNK

    def to_chunks(t):
        return jnp.pad(t, padw).reshape(Bsz, n_chunks, CHUNK, RET_HEADS, -1).transpose(1, 0, 3, 2, 4)

    qc, kc, vc = to_chunks(q), to_chunks(k), to_chunks(v)
    log_gamma = jnp.log(1.0 - 2.0 ** (-5.0 - jnp.arange(RET_HEADS, dtype=jnp.float32)))
    idx = jnp.arange(CHUNK, dtype=jnp.float32)
    intra_decay = jnp.exp(log_gamma[:, None, None] * jnp.abs(idx[:, None] - idx[None, :]))
    q_decay = jnp.exp(log_gamma[:, None] * (idx + 1.0))[..., None]
    k_decay = jnp.exp(log_gamma[:, None] * (CHUNK - 1.0 - idx))[..., None]
    chunk_decay = jnp.exp(log_gamma * CHUNK)[:, None, None]

    def step(S, inp):
        qb, kb, vb = inp
        s = jnp.einsum('bhcd,bhmd->bhcm', qb, kb) * intra_decay
        o = jnp.einsum('bhcm,bhme->bhce', s, vb) + jnp.einsum('bhcd,bhde->bhce', qb * q_decay, S)
        S = S * chunk_decay + jnp.einsum('bhmd,bhme->bhde', kb * k_decay, vb)
        return S, o

    S0 = jnp.zeros((Bsz, RET_HEADS, RET_QK_DIM, RET_V_DIM), jnp.float32)
    _, o = lax.scan(step, S0, (qc, kc, vc))
    o = o.transpose(1, 0, 3, 2, 4).reshape(Bsz, Lp, RET_HEADS, RET_V_DIM)[:, lead:]
    o = rms_norm(o, ret_norm.reshape(RET_HEADS, RET_V_DIM)).reshape(Bsz, L, RET_V_WIDTH).astype(h.dtype)
    return (jax.nn.silu(g) * o) @ w_out


def setup_inputs(seed: int = 0) -> dict:
    key = jax.random.key(seed)
    ks = jax.random.split(key, 24)
    f32 = jnp.float32

    def nrm(k, shape, fan_in):
        return jax.random.normal(k, shape, f32) * (fan_in ** -0.5)

    def gain(k, shape):
        return 1.0 + 0.01 * jax.random.normal(k, shape, f32)

    def small(k, shape):
        return 0.01 * jax.random.normal(k, shape, f32)

    a0 = jax.random.uniform(ks[11], (N_EVEN, LRU_WIDTH), f32, 0.9, 0.999)
    p = a0 ** (1.0 / LRU_C)
    lam = jnp.log(p) - jnp.log1p(-p)
    return {
        "x": jax.random.normal(ks[0], (BATCH, SEQ, D_MODEL), f32),
        "meta_tokens": jax.random.normal(ks[1], (N_META, D_MODEL), f32),
        "ab_norm": gain(ks[2], (N_EVEN, D_MODEL)),
        "ab_w_in": nrm(ks[3], (N_EVEN, D_MODEL, AB_IN), D_MODEL),
        "ab_b_f": 2.0 + 0.1 * jax.random.normal(ks[4], (N_EVEN, FOX_HEADS), f32),
        "ab_conv_w": nrm(ks[5], (N_EVEN, CONV_WIDTH, LRU_WIDTH), CONV_WIDTH),
        "ab_conv_b": small(ks[6], (N_EVEN, LRU_WIDTH)),
        "ab_w_a": nrm(ks[7], (N_EVEN, LRU_BLOCKS, LRU_BLOCK_DIM, LRU_BLOCK_DIM), LRU_BLOCK_DIM),
        "ab_b_a": small(ks[8], (N_EVEN, LRU_WIDTH)),
        "ab_w_x": nrm(ks[9], (N_EVEN, LRU_BLOCKS, LRU_BLOCK_DIM, LRU_BLOCK_DIM), LRU_BLOCK_DIM),
        "ab_b_x": small(ks[10], (N_EVEN, LRU_WIDTH)),
        "ab_lambda": lam,
        "ab_q_norm": gain(ks[12], (N_EVEN, FOX_HEAD_DIM)),
        "ab_k_norm": gain(ks[13], (N_EVEN, FOX_HEAD_DIM)),
        "ab_w_out": nrm(ks[14], (N_EVEN, AB_MIX, D_MODEL), AB_MIX),
        "c_norm": gain(ks[15], (N_ODD, D_MODEL)),
        "c_w_in": nrm(ks[16], (N_ODD, D_MODEL, RET_IN), D_MODEL),
        "c_ret_norm": gain(ks[17], (N_ODD, RET_V_WIDTH)),
        "c_w_out": nrm(ks[18], (N_ODD, RET_V_WIDTH, D_MODEL), RET_V_WIDTH),
        "ffn_norm": gain(ks[19], (DEPTH, D_MODEL)),
        "ffn_w_gate": nrm(ks[20], (DEPTH, D_MODEL, D_FF), D_MODEL),
        "ffn_w_up": nrm(ks[21], (DEPTH, D_MODEL, D_FF), D_MODEL),
        "ffn_w_down": nrm(ks[22], (DEPTH, D_FF, D_MODEL), D_FF),
    }


def reference(x, meta_tokens, ab_norm, ab_w_in, ab_b_f, ab_conv_w, ab_conv_b, ab_w_a, ab_b_a,
              ab_w_x, ab_b_x, ab_lambda, ab_q_norm, ab_k_norm, ab_w_out, c_norm, c_w_in,
              c_ret_norm, c_w_out, ffn_norm, ffn_w_gate, ffn_w_up, ffn_w_down):
    Bsz = x.shape[0]
    meta = jnp.broadcast_to(meta_tokens[None].astype(x.dtype), (Bsz, N_META, D_MODEL))
    h = jnp.concatenate([meta, x], axis=1)
    for layer in range(DEPTH):
        j = layer // 2
        if layer % 2 == 0:
            h = h + lru_fox_mixer(rms_norm(h, ab_norm[j]), ab_w_in[j], ab_b_f[j], ab_conv_w[j],
                                  ab_conv_b[j], ab_w_a[j], ab_b_a[j], ab_w_x[j], ab_b_x[j],
                                  ab_lambda[j], ab_q_norm[j], ab_k_norm[j], ab_w_out[j])
        else:
            h = h + retention_mixer(rms_norm(h, c_norm[j]), c_w_in[j], c_ret_norm[j], c_w_out[j])
        h = h + swiglu(rms_norm(h, ffn_norm[layer]), ffn_w_gate[layer], ffn_w_up[layer], ffn_w_down[layer])
    return h[:, N_META:]
```

```python
import contextlib
import numpy as np
import concourse.bass as bass
import concourse.mybir as mybir
from concourse.bass_utils import run_bass_kernel_spmd


F32 = mybir.dt.float32
BF16 = mybir.dt.bfloat16
AF = mybir.ActivationFunctionType
ALU = mybir.AluOpType
AX = mybir.AxisListType

ENGS = ['pe', 'act', 'dve', 'pool', 'sp']
EPOCH = 8000


class Tile:
    __slots__ = ('name', 'last_w', 'readers')

    def __init__(self, name):
        self.name = name
        self.last_w = None
        self.readers = {}


class Op:
    __slots__ = ('eng', 'fn', 'deps', 'chan', 'flag', 'val', 'idx')

    def __init__(self, eng, fn, deps, chan):
        self.eng = eng
        self.fn = fn
        self.deps = deps
        self.chan = chan
        self.flag = chan is not None
        self.val = 0


class Builder:
    _uid = 0

    def __init__(self, nc):
        self.nc = nc
        self.ops = {e: [] for e in ENGS}
        self.chan_last = {}
        self.chan_eng = {}
        self.nops = 0

    def tile(self, name):
        return Tile(name)

    def tiles(self, name, n):
        return [Tile(f"{name}{i}") for i in range(n)]

    def op(self, eng, fn, reads=(), writes=(), chan=None):
        deps = {}
        for t in reads:
            if t.last_w is not None:
                deps[id(t.last_w)] = t.last_w
        for t in writes:
            if t.last_w is not None:
                deps[id(t.last_w)] = t.last_w
            for r in t.readers.values():
                deps[id(r)] = r
        if chan is not None:
            assert self.chan_eng.setdefault(chan, eng) == eng, chan
            prev = self.chan_last.get(chan)
            if prev is not None:
                deps[id(prev)] = prev
        o = Op(eng, fn, list(deps.values()), chan)
        if chan is not None:
            self.chan_last[chan] = o
        key = ('c', chan) if chan is not None else ('e', eng)
        for t in reads:
            t.readers[key] = o
        for t in writes:
            t.last_w = o
            t.readers = {}
        self.ops[eng].append(o)
        self.nops += 1
        return o

    def dma(self, eng, chan, out, in_, reads=(), writes=(), **kw):
        return self.op(eng, lambda e: e.dma_start(out=out, in_=in_, **kw), reads, writes, chan=chan)

    def mm(self, out, lhsT, rhs, start, stop, reads=(), writes=()):
        return self.op('pe', lambda e: e.matmul(out, lhsT, rhs, start=start, stop=stop), reads, writes)

    def emit(self):
        nc = self.nc
        for e in ENGS:
            for o in self.ops[e]:
                for d in o.deps:
                    if d.chan is None and d.eng == 'pe' and o.eng == 'pe' and o.chan is None:
                        continue
                    d.flag = True
        totals = {}
        for e in ENGS:
            v = 0
            cv = {}
            for o in self.ops[e]:
                if o.chan is not None:
                    cv[o.chan] = cv.get(o.chan, 0) + 1
                    o.val = cv[o.chan]
                elif o.flag:
                    v += 1
                    o.val = v
            totals[('e', e)] = v
            for c, n in cv.items():
                totals[('c', c)] = n
        sems = {}
        for key, n in totals.items():
            for ep in range((n + EPOCH - 1) // EPOCH):
                Builder._uid += 1
                nm = f"s{Builder._uid}_{key[0]}_{key[1]}_{ep}"
                sems[(key, ep)] = nc.alloc_semaphore(name=nm)
        self.n_sems = len(sems)
        with nc.Block() as block:

            def run(ename):
                def body(eh):
                    known = {}
                    for o in self.ops[ename]:
                        for d in o.deps:
                            if d.chan is None and d.eng == 'pe' and ename == 'pe' and o.chan is None:
                                continue
                            key = ('c', d.chan) if d.chan is not None else ('e', d.eng)
                            ep = (d.val - 1) // EPOCH
                            v = (d.val - 1) % EPOCH + 1
                            mult = 16 if d.chan is not None else 1
                            kk = (key, ep)
                            if any(k2[0] == key and k2[1] > ep for k2 in known):
                                continue
                            if known.get(kk, 0) >= v:
                                continue
                            eh.wait_ge(sems[kk], v * mult)
                            known[kk] = v
                        ins = o.fn(eh)
                        if o.flag and ins is not None:
                            if o.chan is not None:
                                key = ('c', o.chan)
                                ins.then_inc(sems[(key, (o.val - 1) // EPOCH)], 16)
                            else:
                                key = ('e', ename)
                                ins.then_inc(sems[(key, (o.val - 1) // EPOCH)], 1)
                    if ename == 'sp':
                        for ch, last in self.chan_last.items():
                            key = ('c', ch)
                            eh.wait_ge(sems[(key, (last.val - 1) // EPOCH)], ((last.val - 1) % EPOCH + 1) * 16)
                return body

            block.tensor(run('pe'))
            block.scalar(run('act'))
            block.vector(run('dve'))
            block.gpsimd(run('pool'))
            block.sync(run('sp'))
        nc.clear_and_free_semaphores(list(sems.values()))
        nc.all_engine_barrier()

    def final_wait(self, eng, ops):
        deps = list(ops)
        o = Op(eng, lambda e: None, deps, None)
        self.ops[eng].append(o)
        return o

import math
import numpy as np
import ml_dtypes

NMETA = 16
EPS = 1e-6


class Cfg:
    def __init__(self, D=4096, SEQ=4096):
        self.D = D
        self.SEQ = SEQ
        self.L = SEQ + NMETA
        self.LW = D // 2
        self.NLB = self.LW // 128
        self.FH = (D // 2) // 128
        self.FW = self.FH * 128
        self.AB_IN = 2 * self.LW + 3 * self.FW + self.FH
        self.AB_MIX = self.LW + self.FW
        self.RH = D // 256
        self.RQK = self.RH * 256
        self.RV = self.RH * 512
        self.RET_IN = 2 * self.RQK + 2 * self.RV
        self.DFF = -(-8 * D // (3 * 256)) * 256
        self.LP = 112 + self.L
        self.NT = self.LP // 128


def split(n, m):
    k = -(-n // m)
    out = []
    s = 0
    for i in range(k):
        e = min(n, s + m)
        out.append((s, e - s))
        s = e
    return out


def tok_blocks(L, maxblk):
    blks = []
    first = min(L, NMETA + (maxblk - NMETA) // 512 * 512) if maxblk >= 512 + NMETA else min(L, maxblk)
    blks.append((0, first))
    s = first
    step = maxblk // 512 * 512 if maxblk >= 512 else maxblk
    while s < L:
        n = min(step, L - s)
        blks.append((s, n))
        s += n
    return blks


def blk_tiles(n):
    if n % 512 == NMETA:
        return split(n - NMETA, 512) + [(n - NMETA, NMETA)] if n > NMETA else [(0, n)]
    return split(n, 512)


class Phase:
    def __init__(self, nc, name):
        self.nc = nc
        self.name = name
        self.st = contextlib.ExitStack()
        self.b = Builder(nc)
        self.ps = []
        self.t_ps = []
        self.pi = 0

    def sb(self, name, shape, dt):
        return self.st.enter_context(self.nc.sbuf_tensor(f"{self.name}_{name}", list(shape), dt))

    def psum(self, n=8):
        for i in range(n):
            self.ps.append(self.st.enter_context(self.nc.psum_tensor(f"{self.name}_ps{i}", [128, 512], F32)))
            self.t_ps.append(self.b.tile(f"ps{i}"))

    def bank(self):
        i = self.pi % len(self.ps)
        self.pi += 1
        return self.ps[i], self.t_ps[i]

    def done(self):
        self.b.emit()
        self.st.close()


def load_consts(ph, dr):
    b = ph.b
    c = {}
    c['ident_f'] = ph.sb("identf", [128, 128], F32)
    c['ident_b'] = ph.sb("identb", [128, 128], BF16)
    c['ones_b'] = ph.sb("onesb", [128, 128], BF16)
    c['t'] = b.tile("consts")
    b.dma('sp', 'cst', c['ident_f'][:, :], dr['c_ident'], writes=[c['t']])
    b.dma('pool', 'cstp', c['ident_b'][:, :], dr['c_ident'], writes=[c['t']])
    b.op('dve', lambda e: e.memset(c['ones_b'][:, :], 1.0), writes=[c['t']])
    return c


def gemm(ph, XT, t_xt, KC, tiles, W, ncols, epi, WB, t_wb, wcols, wstate, groups=1, W2=None, WB2=None, t_wb2=None):
    b = ph.b
    chunks = split(ncols, wcols)
    srcs = [(W, WB, t_wb)] + ([(W2, WB2, t_wb2)] if W2 is not None else [])

    def wload(ci):
        c0, wc = chunks[ci]
        s = (wstate[0] + ci) % 2
        for gi, (Wd, WBd, twd) in enumerate(srcs):
            for pi_, (k0, kn) in enumerate(split(KC, 8)):
                b.dma('pool', f'{ph.name}_w{gi}{s}_{pi_}', WBd[s][:, k0:k0 + kn, 0:wc],
                      Wd[k0 * 128:(k0 + kn) * 128, c0:c0 + wc].rearrange("(kc p) n -> p kc n", p=128), writes=[twd[s][pi_]])

    wload(0)
    for ci, (c0, wc) in enumerate(chunks):
        if ci + 1 < len(chunks):
            wload(ci + 1)
        s = (wstate[0] + ci) % 2
        for cb0, m in split(wc, 128):
            for ti, (t0, n) in enumerate(tiles):
                banks = []
                for gi, (Wd, WBd, twd) in enumerate(srcs):
                    ps, tp = ph.bank()
                    for kc in range(KC):
                        b.mm(ps[0:m, 0:n], WBd[s][:, kc, cb0:cb0 + m], XT[:, kc, t0:t0 + n],
                             start=(kc == 0), stop=(kc == KC - 1), reads=[t_xt[kc // 8], twd[s][kc // 8]], writes=[tp])
                    banks += [ps, tp]
                epi(c0 + cb0, m, ti, t0, n, *banks)
    wstate[0] = (wstate[0] + len(chunks)) % 2


def kparts(KC):
    return len(split(KC, 8))


def norm_prologue(ph, c, hT, gain, tok0, ntok, XT, t_xt, KC, D, ST, t_st, SQ, t_sq, RSB, t_rsb, cnt):
    b = ph.b
    for (s0, sn) in split(ntok, 128):
        i = cnt[0] % 2
        cnt[0] += 1
        st, tst = ST[i], t_st[i]
        for pi_, (k0, kn) in enumerate(split(KC, 8)):
            b.dma('sp', f'{ph.name}_st{i}_{pi_}', st[:, k0:k0 + kn, 0:sn],
                  hT[k0 * 128:(k0 + kn) * 128, tok0 + s0:tok0 + s0 + sn].rearrange("(kc p) t -> p kc t", p=128), writes=[tst[pi_]])
        b.op('act', lambda e, st=st, sn=sn: e.activation(SQ[:, :, 0:sn], st[:, :, 0:sn], AF.Square),
             reads=tst, writes=[t_sq])
        ps, tp = ph.bank()
        for kc in range(KC):
            b.mm(ps[:, 0:sn], c['ones_b'][:, :], SQ[:, kc, 0:sn], start=(kc == 0), stop=(kc == KC - 1),
                 reads=[t_sq, c['t']], writes=[tp])
        b.op('act', lambda e, ps=ps, s0=s0, sn=sn: e.activation(RSB[:, s0:s0 + sn], ps[:, 0:sn], AF.Sqrt, bias=c['eps'][:, 0:1], scale=1.0 / D),
             reads=[tp, c['t']], writes=[t_rsb])
        b.op('dve', lambda e, s0=s0, sn=sn: e.reciprocal(RSB[:, s0:s0 + sn], RSB[:, s0:s0 + sn]), reads=[t_rsb], writes=[t_rsb])
        b.op('dve', lambda e, st=st, s0=s0, sn=sn: e.tensor_tensor(
            XT[:, :, s0:s0 + sn], st[:, :, 0:sn], gain[:, :].unsqueeze(2).broadcast_to([128, KC, sn]), ALU.mult),
            reads=list(tst) + [c['t']], writes=list(t_xt))


def load_cols(ph, c, name, vec, K):
    t = ph.sb(name, [128, K // 128], F32)
    ph.b.dma('sp', 'cst', t[:, :], vec.rearrange("(k p) -> p k", p=128), writes=[c['t']], allow_slow_non_contiguous=True)
    return t


def make_eps(ph, c):
    c['eps'] = ph.sb("eps", [128, 1], F32)
    ph.b.op('dve', lambda e: e.memset(c['eps'][:, :], EPS), writes=[c['t']])


def phase_prep(nc, cfg, dr):
    ph = Phase(nc, "p0")
    b = ph.b
    D, L = cfg.D, cfg.L
    KC = D // 128
    ph.psum(8)
    c = load_consts(ph, dr)
    XI = [ph.sb(f"xi{i}", [128, D], F32) for i in range(2)]
    t_xi = b.tiles("xi", 2)
    XO = [ph.sb(f"xo{i}", [128, KC, 128], F32) for i in range(2)]
    t_xo = b.tiles("xo", 2)
    ttiles = [('m', 0, NMETA)] + [('x', s, n) for (s, n) in split(cfg.SEQ, 128)]
    for i, (kind, s, n) in enumerate(ttiles):
        xi, txi, xo, txo = XI[i % 2], t_xi[i % 2], XO[i % 2], t_xo[i % 2]
        src = dr['meta'][0:n, :] if kind == 'm' else dr['x'][s:s + n, :]
        b.dma('pool', f'p0_xi{i%2}', xi[0:n, :], src, writes=[txi])
        for g in range(KC // 4):
            ps, tp = ph.bank()
            for q in range(4):
                kc = g * 4 + q
                b.op('pe', lambda e, ps=ps, q=q, xi=xi, kc=kc, n=n: e.transpose(
                    ps[:, q * 128:q * 128 + n], xi[0:n, kc * 128:(kc + 1) * 128], c['ident_f'][0:n, 0:n]),
                    reads=[txi, c['t']], writes=[tp])
            eng = 'act' if g % 2 == 0 else 'dve'
            if eng == 'act':
                b.op('act', lambda e, ps=ps, xo=xo, g=g, n=n: e.activation(
                    xo[:, g * 4:g * 4 + 4, 0:n], ps[:, :].rearrange("p (q t) -> p q t", q=4)[:, :, 0:n], AF.Copy),
                    reads=[tp], writes=[txo])
            else:
                b.op('dve', lambda e, ps=ps, xo=xo, g=g, n=n: e.tensor_copy(
                    xo[:, g * 4:g * 4 + 4, 0:n], ps[:, :].rearrange("p (q t) -> p q t", q=4)[:, :, 0:n]),
                    reads=[tp], writes=[txo])
        tok0 = 0 if kind == 'm' else NMETA + s
        for pi_, (k0, kn) in enumerate(split(KC, 8)):
            b.dma('sp', f'p0_xo{i%2}_{pi_}', dr['hT'][k0 * 128:(k0 + kn) * 128, tok0:tok0 + n].rearrange("(kc p) t -> p kc t", p=128),
                  xo[:, k0:k0 + kn, 0:n], reads=[txo])
    ph.done()


def phase_inproj(nc, cfg, dr, name, gain_vec, W, ncols, zT):
    ph = Phase(nc, name)
    b = ph.b
    D, L = cfg.D, cfg.L
    KC = D // 128
    ph.psum(8)
    c = load_consts(ph, dr)
    make_eps(ph, c)
    gain = load_cols(ph, c, "gain", gain_vec, D)
    TB = 1040
    XT = ph.sb("XT", [128, KC, TB], BF16)
    t_xt = b.tiles("xt", kparts(KC))
    wcols = 512
    WB = [ph.sb(f"wb{i}", [128, KC, wcols], BF16) for i in range(2)]
    t_wb = [b.tiles(f"wb{i}_", kparts(KC)) for i in range(2)]
    ST = [ph.sb(f"st{i}", [128, KC, 128], F32) for i in range(2)]
    t_st = [b.tiles(f"st{i}_", kparts(KC)) for i in range(2)]
    SQ = ph.sb("sq", [128, KC, 128], BF16)
    t_sq = b.tile("sq")
    RSB = ph.sb("rsb", [128, TB], F32)
    t_rsb = b.tile("rsb")
    OB = [ph.sb(f"ob{i}", [128, 512], F32) for i in range(4)]
    t_ob = b.tiles("ob", 4)
    cnt = [0]
    ocnt = [0]
    wstate = [0]
    for (tok0, ntok) in tok_blocks(L, TB):
        norm_prologue(ph, c, dr['hT'], gain, tok0, ntok, XT, t_xt, KC, D, ST, t_st, SQ, t_sq, RSB, t_rsb, cnt)

        def epi(c0, m, ti, t0, n, ps, tp, tok0=tok0):
            i = ocnt[0] % 4
            ocnt[0] += 1
            ob, tob = OB[i], t_ob[i]
            b.op('dve', lambda e: e.tensor_tensor(ob[0:m, 0:n], ps[0:m, 0:n], RSB[0:m, t0:t0 + n], ALU.mult),
                 reads=[tp, t_rsb], writes=[tob])
            for (r0, r1, zap) in zT:
                if r0 <= c0 < r1:
                    b.dma('sp', f'{name}_o{i}', zap[c0 - r0:c0 - r0 + m, tok0 + t0:tok0 + t0 + n], ob[0:m, 0:n], reads=[tob])

        gemm(ph, XT, t_xt, KC, blk_tiles(ntok), W, ncols, epi, WB, t_wb, wcols, wstate)
    ph.done()


def phase_outproj(nc, cfg, dr, name, yT, K, W, TB, wcols):
    ph = Phase(nc, name)
    b = ph.b
    D, L = cfg.D, cfg.L
    KC = K // 128
    ph.psum(8)
    XT = ph.sb("XT", [128, KC, TB], BF16)
    t_xt = b.tiles("xt", kparts(KC))
    WB = [ph.sb(f"wb{i}", [128, KC, wcols], BF16) for i in range(2)]
    t_wb = [b.tiles(f"wb{i}_", kparts(KC)) for i in range(2)]
    RB = [ph.sb(f"rb{i}", [128, 512], F32) for i in range(4)]
    t_rb = b.tiles("rb", 4)
    ocnt = [0]
    wstate = [0]
    for (tok0, ntok) in tok_blocks(L, TB):
        for pi_, (k0, kn) in enumerate(split(KC, 8)):
            b.dma('sp', f'{name}_x{pi_}', XT[:, k0:k0 + kn, 0:ntok],
                  yT[k0 * 128:(k0 + kn) * 128, tok0:tok0 + ntok].rearrange("(kc p) t -> p kc t", p=128), writes=[t_xt[pi_]])

        def epi(c0, m, ti, t0, n, ps, tp, tok0=tok0):
            i = ocnt[0] % 4
            ocnt[0] += 1
            rb, trb = RB[i], t_rb[i]
            dst = dr['hT'][c0:c0 + m, tok0 + t0:tok0 + t0 + n]
            b.dma('sp', f'{name}_r{i}', rb[0:m, 0:n], dst, writes=[trb])
            b.op('dve', lambda e: e.tensor_tensor(rb[0:m, 0:n], ps[0:m, 0:n], rb[0:m, 0:n], ALU.add),
                 reads=[tp, trb], writes=[trb])
            b.dma('sp', f'{name}_r{i}', dst, rb[0:m, 0:n], reads=[trb])

        gemm(ph, XT, t_xt, KC, blk_tiles(ntok), W, D, epi, WB, t_wb, wcols, wstate)
    ph.done()


def phase_ffn_gu(nc, cfg, dr, name, gain_vec, Wg, Wu):
    ph = Phase(nc, name)
    b = ph.b
    D, L = cfg.D, cfg.L
    KC = D // 128
    ph.psum(8)
    c = load_consts(ph, dr)
    make_eps(ph, c)
    gain = load_cols(ph, c, "gain", gain_vec, D)
    TB = 1040
    XT = ph.sb("XT", [128, KC, TB], BF16)
    t_xt = b.tiles("xt", kparts(KC))
    wcols = 256
    WB = [ph.sb(f"wg{i}", [128, KC, wcols], BF16) for i in range(2)]
    t_wb = [b.tiles(f"wg{i}_", kparts(KC)) for i in range(2)]
    WB2 = [ph.sb(f"wu{i}", [128, KC, wcols], BF16) for i in range(2)]
    t_wb2 = [b.tiles(f"wu{i}_", kparts(KC)) for i in range(2)]
    ST = [ph.sb(f"st{i}", [128, KC, 128], F32) for i in range(2)]
    t_st = [b.tiles(f"st{i}_", kparts(KC)) for i in range(2)]
    SQ = ph.sb("sq", [128, KC, 128], BF16)
    t_sq = b.tile("sq")
    RSB = ph.sb("rsb", [128, TB], F32)
    t_rsb = b.tile("rsb")
    SG = [ph.sb(f"sg{i}", [128, 512], F32) for i in range(2)]
    t_sg = b.tiles("sg", 2)
    HB = [ph.sb(f"hb{i}", [128, 512], BF16) for i in range(4)]
    t_hb = b.tiles("hb", 4)
    cnt = [0]
    ocnt = [0]
    wstate = [0]
    for (tok0, ntok) in tok_blocks(L, TB):
        norm_prologue(ph, c, dr['hT'], gain, tok0, ntok, XT, t_xt, KC, D, ST, t_st, SQ, t_sq, RSB, t_rsb, cnt)

        def epi(c0, m, ti, t0, n, psg, tpg, psu, tpu, tok0=tok0):
            i = ocnt[0] % 4
            ocnt[0] += 1
            sg, tsg = SG[i % 2], t_sg[i % 2]
            hb, thb = HB[i], t_hb[i]
            b.op('dve', lambda e: e.tensor_tensor(sg[0:m, 0:n], psg[0:m, 0:n], RSB[0:m, t0:t0 + n], ALU.mult),
                 reads=[tpg, t_rsb], writes=[tsg])
            b.op('act', lambda e: e.activation(sg[0:m, 0:n], sg[0:m, 0:n], AF.Silu), reads=[tsg], writes=[tsg])
            b.op('dve', lambda e: e.tensor_tensor(sg[0:m, 0:n], sg[0:m, 0:n], RSB[0:m, t0:t0 + n], ALU.mult),
                 reads=[tsg, t_rsb], writes=[tsg])
            b.op('dve', lambda e: e.tensor_tensor(hb[0:m, 0:n], psu[0:m, 0:n], sg[0:m, 0:n], ALU.mult),
                 reads=[tpu, tsg], writes=[thb])
            b.dma('sp', f'{name}_o{i}', dr['HT'][c0:c0 + m, tok0 + t0:tok0 + t0 + n], hb[0:m, 0:n], reads=[thb])

        gemm(ph, XT, t_xt, KC, blk_tiles(ntok), Wg, cfg.DFF, epi, WB, t_wb, wcols, wstate,
             W2=Wu, WB2=WB2, t_wb2=t_wb2)
    ph.done()


def phase_lru(nc, cfg, dr):
    ph = Phase(nc, "lru")
    b = ph.b
    L, LW, NLB = cfg.L, cfg.LW, cfg.NLB
    ph.psum(4)
    c = load_consts(ph, dr)
    zT = dr['zT0']
    cw = ph.sb("cw", [128, 4, NLB], F32)
    b.dma('sp', 'cst', cw[:, :, :], dr['ab_conv_w'].rearrange("j (n p) -> p j n", p=128), writes=[c['t']], allow_slow_non_contiguous=True)
    cb = load_cols(ph, c, "cb", dr['ab_conv_b'], LW)
    ba = load_cols(ph, c, "ba", dr['ab_b_a'], LW)
    bx = load_cols(ph, c, "bx", dr['ab_b_x'], LW)
    lam = load_cols(ph, c, "lam", dr['ab_lambda'], LW)
    cl = ph.sb("cl", [128, NLB], F32)
    cl2 = ph.sb("cl2", [128, NLB], F32)
    b.op('act', lambda e: e.activation(cl[:, :], lam[:, :], AF.Exp, scale=-1.0), reads=[c['t']], writes=[c['t']])
    b.op('dve', lambda e: e.tensor_scalar_add(cl[:, :], cl[:, :], 1.0), reads=[c['t']], writes=[c['t']])
    b.op('act', lambda e: e.activation(cl[:, :], cl[:, :], AF.Ln), reads=[c['t']], writes=[c['t']])
    b.op('dve', lambda e: e.tensor_scalar_mul(cl2[:, :], cl[:, :], -16.0), reads=[c['t']], writes=[c['t']])
    b.op('dve', lambda e: e.tensor_scalar_mul(cl[:, :], cl[:, :], -8.0), reads=[c['t']], writes=[c['t']])
    Wa = ph.sb("Wa", [128, NLB, 128], BF16)
    Wx = ph.sb("Wx", [128, NLB, 128], BF16)
    b.dma('pool', 'cstp', Wa[:, :, :], dr['ab_w_a'].rearrange("n c d -> c n d"), writes=[c['t']])
    b.dma('pool', 'cstp', Wx[:, :, :], dr['ab_w_x'].rearrange("n c d -> c n d"), writes=[c['t']])

    xp = ph.sb("xp", [128, 3 + L], F32); t_xp = b.tile("xp")
    gt = ph.sb("gt", [128, L], F32); t_gt = b.tile("gt")
    xc = ph.sb("xc", [128, L], F32); t_xc = b.tile("xc")
    xcb = ph.sb("xcb", [128, L], BF16); t_xcb = b.tile("xcb")
    rr = ph.sb("rr", [128, L], F32); t_rr = b.tile("rr")
    ii = ph.sb("ii", [128, L], F32); t_ii = b.tile("ii")
    t1 = ph.sb("t1", [128, L], F32); t_t1 = b.tile("t1")
    yb = ph.sb("yb", [128, L], BF16); t_yb = b.tile("yb")
    b.op('dve', lambda e: e.memset(xp[:, 0:3], 0.0), writes=[t_xp])
    for n in range(NLB):
        b.dma('sp', 'lru_x', xp[:, 3:3 + L], zT[n * 128:(n + 1) * 128, :], writes=[t_xp])
        b.dma('sp', 'lru_g', gt[:, :], zT[LW + n * 128:LW + (n + 1) * 128, :], writes=[t_gt])
        b.op('dve', lambda e, n=n: e.tensor_scalar(xc[:, :], xp[:, 3:3 + L], cw[:, 3, n:n + 1], cb[:, n:n + 1], ALU.mult, ALU.add),
             reads=[t_xp, c['t']], writes=[t_xc])
        for j in range(3):
            b.op('dve', lambda e, n=n, j=j: e.scalar_tensor_tensor(xc[:, :], xp[:, j:j + L], cw[:, j, n:n + 1], xc[:, :], ALU.mult, ALU.add),
                 reads=[t_xp, t_xc, c['t']], writes=[t_xc])
        b.op('act', lambda e: e.activation(xcb[:, :], xc[:, :], AF.Copy), reads=[t_xc], writes=[t_xcb])
        for (t0, tn) in split(L, 512):
            ps, tp = ph.bank()
            b.mm(ps[:, 0:tn], Wa[:, n, :], xcb[:, t0:t0 + tn], True, True, reads=[t_xcb, c['t']], writes=[tp])
            b.op('act', lambda e, ps=ps, t0=t0, tn=tn, n=n: e.activation(rr[:, t0:t0 + tn], ps[:, 0:tn], AF.Sigmoid, bias=ba[:, n:n + 1]),
                 reads=[tp, c['t']], writes=[t_rr])
            ps, tp = ph.bank()
            b.mm(ps[:, 0:tn], Wx[:, n, :], xcb[:, t0:t0 + tn], True, True, reads=[t_xcb, c['t']], writes=[tp])
            b.op('act', lambda e, ps=ps, t0=t0, tn=tn, n=n: e.activation(ii[:, t0:t0 + tn], ps[:, 0:tn], AF.Sigmoid, bias=bx[:, n:n + 1]),
                 reads=[tp, c['t']], writes=[t_ii])
        b.op('act', lambda e, n=n: e.activation(t1[:, :], rr[:, :], AF.Exp, scale=cl2[:, n:n + 1]), reads=[t_rr, c['t']], writes=[t_t1])
        b.op('dve', lambda e: e.tensor_scalar(t1[:, :], t1[:, :], -1.0, 1.0, ALU.mult, ALU.add), reads=[t_t1], writes=[t_t1])
        b.op('act', lambda e: e.activation(t1[:, :], t1[:, :], AF.Sqrt), reads=[t_t1], writes=[t_t1])
        b.op('dve', lambda e: e.tensor_tensor(ii[:, :], ii[:, :], xc[:, :], ALU.mult), reads=[t_ii, t_xc], writes=[t_ii])
        b.op('dve', lambda e: e.tensor_tensor(ii[:, :], ii[:, :], t1[:, :], ALU.mult), reads=[t_ii, t_t1], writes=[t_ii])
        b.op('act', lambda e, n=n: e.activation(rr[:, :], rr[:, :], AF.Exp, scale=cl[:, n:n + 1]), reads=[t_rr, c['t']], writes=[t_rr])
        b.op('dve', lambda e: e.tensor_tensor_scan(t1[:, :], rr[:, :], ii[:, :], 0.0, ALU.mult, ALU.add),
             reads=[t_rr, t_ii], writes=[t_t1])
        b.op('act', lambda e: e.activation(xc[:, :], gt[:, :], AF.Square), reads=[t_gt], writes=[t_xc])
        b.op('dve', lambda e: e.tensor_scalar(xc[:, :], xc[:, :], 0.044715, 1.0, ALU.mult, ALU.add), reads=[t_xc], writes=[t_xc])
        b.op('dve', lambda e: e.tensor_tensor(xc[:, :], xc[:, :], gt[:, :], ALU.mult), reads=[t_xc, t_gt], writes=[t_xc])
        b.op('act', lambda e: e.activation(xc[:, :], xc[:, :], AF.Sigmoid, scale=1.5957691216057308), reads=[t_xc], writes=[t_xc])
        b.op('dve', lambda e: e.tensor_tensor(xc[:, :], xc[:, :], gt[:, :], ALU.mult), reads=[t_xc, t_gt], writes=[t_xc])
        b.op('dve', lambda e: e.tensor_tensor(yb[:, :], t1[:, :], xc[:, :], ALU.mult), reads=[t_t1, t_xc], writes=[t_yb])
        b.dma('sp', 'lru_y', dr['yT0'][n * 128:(n + 1) * 128, :], yb[:, :], reads=[t_yb])
    ph.done()


def phase_fox(nc, cfg, dr):
    ph = Phase(nc, "fox")
    b = ph.b
    L, LW, FH, FW = cfg.L, cfg.LW, cfg.FH, cfg.FW
    ph.psum(8)
    PS_S = list(zip(ph.ps[0:4], ph.t_ps[0:4]))
    PS_O = list(zip(ph.ps[4:6], ph.t_ps[4:6]))
    PS_D = list(zip(ph.ps[6:8], ph.t_ps[6:8]))
    scnt = [0]

    def sbank():
        i = scnt[0] % 4
        scnt[0] += 1
        return PS_S[i]

    c = load_consts(ph, dr)
    make_eps(ph, c)
    zT = dr['zT0']
    q0r = 2 * LW
    k0r = 2 * LW + FW
    v0r = 2 * LW + 2 * FW
    f0r = 2 * LW + 3 * FW
    KT = split(L, 128)
    NKT = len(KT)
    SCALE = 128 ** -0.5
    qg = ph.sb("qg", [128, 1], F32)
    kg = ph.sb("kg", [128, 1], F32)
    b.dma('sp', 'cst', qg[:, :], dr['ab_q_norm'].rearrange("(p o) -> p o", o=1), writes=[c['t']], allow_slow_non_contiguous=True)
    b.dma('sp', 'cst', kg[:, :], dr['ab_k_norm'].rearrange("(p o) -> p o", o=1), writes=[c['t']], allow_slow_non_contiguous=True)
    nbf = ph.sb("nbf", [FH, 1], F32)
    b.dma('sp', 'cst', nbf[:, :], dr['ab_b_f'].rearrange("(p o) -> p o", o=1), writes=[c['t']], allow_slow_non_contiguous=True)
    b.op('dve', lambda e: e.tensor_scalar_mul(nbf[:, :], nbf[:, :], -1.0), reads=[c['t']], writes=[c['t']])
    maskb = ph.sb("maskb", [128, 4, 512], BF16)
    b.dma('pool', 'cstp', maskb[:, :, :], dr['c_foxmask'], writes=[c['t']])
    onesK = ph.sb("onesK", [128, 128], BF16)
    b.op('dve', lambda e: e.memset(onesK[:, :], 0.0), writes=[c['t']])
    b.op('dve', lambda e: e.memset(onesK[0:3, :], 1.0), writes=[c['t']])

    QF = [ph.sb(f"qf{i}", [128, L], F32) for i in range(2)]; t_QF = b.tiles("qf", 2)
    KF = [ph.sb(f"kf{i}", [128, L], F32) for i in range(2)]; t_KF = b.tiles("kf", 2)
    VFF = [ph.sb(f"vf{i}", [128, L], F32) for i in range(2)]; t_VFF = b.tiles("vf", 2)
    SQB = [ph.sb(f"sqb{i}", [128, 512], BF16) for i in range(2)]; t_SQB = b.tiles("sqb", 2)
    QN = [ph.sb(f"qn{i}", [128, L], BF16) for i in range(2)]; t_QN = b.tiles("qn", 2)
    KN = [ph.sb(f"kn{i}", [128, L], BF16) for i in range(2)]; t_KN = b.tiles("kn", 2)
    VT = [ph.sb(f"Vt{i}", [128, NKT, 128], BF16) for i in range(2)]; t_VT = b.tiles("vt", 2)
    negck = ph.sb("negck", [128, NKT, FH], F32); t_nck = b.tile("negck")
    CQ3 = [ph.sb(f"cq3{i}", [128, L], BF16) for i in range(2)]; t_CQ3 = b.tiles("cq3", 2)
    RSS = [ph.sb(f"rs{i}", [128, 512], F32) for i in range(2)]; t_RSS = b.tiles("rs", 2)
    PT = [ph.sb(f"pt{i}", [128, 512], BF16) for i in range(3)]; t_pt = b.tiles("pt", 3)
    REC = ph.sb("rec", [128, 512], F32); t_rec = b.tile("rec")
    OB = [ph.sb(f"ob{i}", [128, 512], BF16) for i in range(2)]; t_ob = b.tiles("ob", 2)

    fl, cum, one, sqb = QF[0], KF[0], VFF[0], QN[0]
    t_fl, t_cum, t_one, t_sqb = t_QF[0], t_KF[0], t_VFF[0], t_QN[0]
    b.dma('sp', 'fox_q0', fl[0:FH, :], zT[f0r:f0r + FH, :], writes=[t_fl])
    b.op('act', lambda e: e.activation(fl[0:FH, :], fl[0:FH, :], AF.Exp, bias=nbf[:, 0:1], scale=-1.0), reads=[t_fl, c['t']], writes=[t_fl])
    b.op('dve', lambda e: e.tensor_scalar_add(fl[0:FH, :], fl[0:FH, :], 1.0), reads=[t_fl], writes=[t_fl])
    b.op('act', lambda e: e.activation(fl[0:FH, :], fl[0:FH, :], AF.Ln), reads=[t_fl], writes=[t_fl])
    b.op('dve', lambda e: e.tensor_scalar_mul(fl[0:FH, :], fl[0:FH, :], -1.0), reads=[t_fl], writes=[t_fl])
    b.op('dve', lambda e: e.memset(one[0:FH, :], 1.0), writes=[t_one])
    b.op('dve', lambda e: e.tensor_tensor_scan(cum[0:FH, :], one[0:FH, :], fl[0:FH, :], 0.0, ALU.mult, ALU.add),
         reads=[t_one, t_fl], writes=[t_cum])
    for kt, (k0, kn_) in enumerate(KT):
        ps, tp = sbank()
        b.op('pe', lambda e, ps=ps, k0=k0, kn_=kn_: e.transpose(ps[0:kn_, 0:FH], cum[0:FH, k0:k0 + kn_], c['ident_f'][0:FH, 0:FH]),
             reads=[t_cum, c['t']], writes=[tp])
        b.op('dve', lambda e, ps=ps, kt=kt, kn_=kn_: e.tensor_scalar_mul(negck[0:kn_, kt, :], ps[0:kn_, 0:FH], -1.0),
             reads=[tp], writes=[t_nck])
    t_cqd = b.tile("cqD")
    b.op('dve', lambda e: e.tensor_scalar_mul(cum[0:FH, :], cum[0:FH, :], 128 ** 0.5), reads=[t_cum], writes=[t_cum])
    for part in range(3):
        b.op('dve', lambda e: e.tensor_copy(sqb[0:FH, :], cum[0:FH, :]), reads=[t_cum], writes=[t_sqb])
        b.dma('sp', 'fox_cqw', dr['cqD'][:, part, :], sqb[0:FH, :], reads=[t_sqb], writes=[t_cqd])
        if part < 2:
            b.op('dve', lambda e: e.tensor_tensor(cum[0:FH, :], cum[0:FH, :], sqb[0:FH, :], ALU.subtract),
                 reads=[t_cum, t_sqb], writes=[t_cum])
    for i in range(2):
        b.op('dve', lambda e, i=i: e.memset(CQ3[i][:, :], 0.0), writes=[t_CQ3[i]])

    QT = split(L, 512)
    ptc = [0]

    def loads(h):
        s_ = h % 2
        b.dma('sp', f'fox_q{s_}', QF[s_][:, :], zT[q0r + h * 128:q0r + (h + 1) * 128, :], writes=[t_QF[s_]])
        b.dma('sp', f'fox_k{s_}', KF[s_][:, :], zT[k0r + h * 128:k0r + (h + 1) * 128, :], writes=[t_KF[s_]])
        b.dma('sp', f'fox_v{s_}', VFF[s_][:, :], zT[v0r + h * 128:v0r + (h + 1) * 128, :], writes=[t_VFF[s_]])
        b.dma('sp', f'fox_cq{s_}', CQ3[s_][0:3, :], dr['cqD'][h, :, :], reads=[t_cqd], writes=[t_CQ3[s_]])

    def prologue_step(h, j):
        s_ = h % 2
        t0, tn = QT[j]
        for xi, (xf, t_xf, xn, t_xn, g) in enumerate(((QF[s_], t_QF[s_], QN[s_], t_QN[s_], qg), (KF[s_], t_KF[s_], KN[s_], t_KN[s_], kg))):
            sq, tsq, rs, trs = SQB[xi], t_SQB[xi], RSS[xi], t_RSS[xi]
            b.op('act', lambda e, xf=xf, sq=sq, t0=t0, tn=tn: e.activation(sq[:, 0:tn], xf[:, t0:t0 + tn], AF.Square), reads=[t_xf], writes=[tsq])
            ps, tp = sbank()
            b.mm(ps[:, 0:tn], c['ones_b'][:, :], sq[:, 0:tn], True, True, reads=[tsq, c['t']], writes=[tp])
            b.op('act', lambda e, ps=ps, rs=rs, tn=tn: e.activation(rs[:, 0:tn], ps[:, 0:tn], AF.Sqrt, bias=c['eps'][:, 0:1], scale=1.0 / 128),
                 reads=[tp, c['t']], writes=[trs])
            b.op('dve', lambda e, rs=rs, tn=tn: e.reciprocal(rs[:, 0:tn], rs[:, 0:tn]), reads=[trs], writes=[trs])
            b.op('dve', lambda e, xf=xf, xn=xn, g=g, rs=rs, t0=t0, tn=tn: e.scalar_tensor_tensor(
                xn[:, t0:t0 + tn], xf[:, t0:t0 + tn], g[:, 0:1], rs[:, 0:tn], ALU.mult, ALU.mult),
                reads=[t_xf, trs, c['t']], writes=[t_xn])
        for kt, (k0, kn_) in enumerate(KT):
            if not (t0 <= k0 < t0 + tn):
                continue
            ps, tp = sbank()
            b.op('pe', lambda e, ps=ps, k0=k0, kn_=kn_, s_=s_: e.transpose(ps[0:kn_, 0:128], VFF[s_][:, k0:k0 + kn_], c['ident_f'][:, :]),
                 reads=[t_VFF[s_], c['t']], writes=[tp])
            b.op('act', lambda e, ps=ps, kt=kt, kn_=kn_, s_=s_: e.activation(VT[s_][0:kn_, kt, :], ps[0:kn_, 0:128], AF.Copy),
                 reads=[tp], writes=[t_VT[s_]])

    def attention(h, qi):
        s_ = h % 2
        kn, qn, Vt, cq3 = KN[s_], QN[s_], VT[s_], CQ3[s_]
        t_kn, t_qn, t_vt, t_cq3 = t_KN[s_], t_QN[s_], t_VT[s_], t_CQ3[s_]
        q0, qn_ = QT[qi]
        kts = [kt for kt, (k0, kn_) in enumerate(KT) if k0 <= q0 + qn_ - 1]
        psO, tpO = PS_O[qi % 2]
        psD, tpD = PS_D[qi % 2]

        def emit_S(kt):
            k0, kn_ = KT[kt]
            ps, tp = sbank()
            diag = (k0 + kn_ - 1 > q0)
            b.mm(ps[0:kn_, 0:qn_], kn[:, k0:k0 + kn_], qn[:, q0:q0 + qn_], True, False, reads=[t_kn, t_qn], writes=[tp])
            b.mm(ps[0:kn_, 0:qn_], onesK[:, 0:kn_], cq3[:, q0:q0 + qn_], False, not diag, reads=[t_cq3, c['t']], writes=[tp])
            if diag:
                mi = (k0 - q0) // 128
                b.mm(ps[0:kn_, 0:qn_], c['ident_b'][:, 0:kn_], maskb[:, mi, 0:qn_], False, True, reads=[c['t']], writes=[tp])
            return ps, tp

        pend = [emit_S(kts[i]) for i in range(min(2, len(kts)))]
        for idx, kt in enumerate(kts):
            k0, kn_ = KT[kt]
            ps, tp = pend.pop(0)
            if idx + 2 < len(kts):
                pend.append(emit_S(kts[idx + 2]))
            pi = ptc[0] % 3
            ptc[0] += 1
            pt, tpt = PT[pi], t_pt[pi]
            b.op('act', lambda e, ps=ps, pt=pt, kt=kt, kn_=kn_, h=h, qn_=qn_: e.activation(
                pt[0:kn_, 0:qn_], ps[0:kn_, 0:qn_], AF.Exp, bias=negck[0:kn_, kt, h:h + 1], scale=SCALE),
                reads=[tp, t_nck], writes=[tpt])
            b.mm(psO[:, 0:qn_], Vt[0:kn_, kt, :], pt[0:kn_, 0:qn_], idx == 0, idx == len(kts) - 1, reads=[t_vt, tpt], writes=[tpO])
            b.mm(psD[:, 0:qn_], c['ones_b'][0:kn_, :], pt[0:kn_, 0:qn_], idx == 0, idx == len(kts) - 1, reads=[c['t'], tpt], writes=[tpD])
        b.op('dve', lambda e, psD=psD, qn_=qn_: e.reciprocal(REC[:, 0:qn_], psD[:, 0:qn_]), reads=[tpD], writes=[t_rec])
        ob, tob = OB[qi % 2], t_ob[qi % 2]
        b.op('dve', lambda e, psO=psO, ob=ob, qn_=qn_: e.tensor_tensor(ob[:, 0:qn_], psO[:, 0:qn_], REC[:, 0:qn_], ALU.mult),
             reads=[tpO, t_rec], writes=[tob])
        b.dma('sp', f'fox_o{qi%2}', dr['yT0'][LW + h * 128:LW + (h + 1) * 128, q0:q0 + qn_], ob[:, 0:qn_], reads=[tob])

    loads(0)
    for j in range(len(QT)):
        prologue_step(0, j)
    for h in range(FH):
        if h + 1 < FH:
            loads(h + 1)
        for qi in range(len(QT)):
            attention(h, qi)
            if h + 1 < FH:
                prologue_step(h + 1, qi)
    ph.done()


def phase_ret(nc, cfg, dr):
    ph = Phase(nc, "ret")
    b = ph.b
    L, LP, NT, RH, RQK, RV = cfg.L, cfg.LP, cfg.NT, cfg.RH, cfg.RQK, cfg.RV
    ph.psum(6)
    PSO = [(ph.ps[4], ph.t_ps[4]), (ph.ps[5], ph.t_ps[5])]
    ph.ps, ph.t_ps = ph.ps[:4], ph.t_ps[:4]
    PST = ph.st.enter_context(nc.psum_tensor("ret_pst", [128, 1024], F32))
    t_pst = b.tile("pst")
    c = load_consts(ph, dr)
    make_eps(ph, c)
    cosT = ph.sb("cosT", [128, LP], F32)
    sinT = ph.sb("sinT", [128, LP], F32)
    b.dma('sp', 'cst', cosT[:, :], dr['c_cos'], writes=[c['t']])
    b.dma('sp', 'cst', sinT[:, :], dr['c_sin'], writes=[c['t']])
    Dm = ph.sb("Dm", [128, RH, 128], F32)
    tq = ph.sb("tq", [128, RH, 128], F32)
    kd = ph.sb("kd", [128, RH], F32)
    b.dma('sp', 'cst', Dm[:, :, :], dr['c_Dm'], writes=[c['t']])
    b.dma('sp', 'cst', tq[:, :, :], dr['c_tq'], writes=[c['t']])
    b.dma('sp', 'cst', kd[:, :], dr['c_kd'], writes=[c['t']])
    rg = load_cols(ph, c, "rg", dr['c_ret_norm'], RV)

    RC = 1056
    XC = [[ph.sb(f"xc{j}{i}", [128, RC], F32) for i in range(2)] for j in range(2)]
    t_xc = [b.tiles(f"xc{j}_", 2) for j in range(2)]
    TA = ph.sb("ta", [128, RC], F32); t_ta = b.tile("ta")
    TB_ = ph.sb("tb", [128, RC], F32); t_tb = b.tile("tb")
    Qb = [ph.sb(f"qb{i}", [128, 2, LP], BF16) for i in range(2)]; t_qb = b.tiles("qb", 2)
    Kb = [ph.sb(f"kb{i}", [128, 2, LP], BF16) for i in range(2)]; t_kb = b.tiles("kb", 2)
    VF = [ph.sb(f"vf{i}", [128, 4, 384], F32) for i in range(2)]; t_vf = b.tiles("vf", 2)
    GF = [ph.sb(f"gf{i}", [128, 4, 384], F32) for i in range(2)]; t_gf = b.tiles("gf", 2)
    S = ph.sb("S", [128, 2, 512], F32); t_S = b.tile("S")
    Sb = ph.sb("Sb", [128, 2, 512], BF16); t_Sb = b.tile("Sb")
    sTb = [ph.sb(f"sTb{i}", [128, 128], BF16) for i in range(2)]; t_sTb = b.tiles("sTb", 2)
    Vb = [ph.sb(f"Vb{i}", [128, 512], BF16) for i in range(2)]; t_Vb = b.tiles("Vb", 2)
    Kdb = [ph.sb(f"Kdb{i}", [128, 256], BF16) for i in range(2)]; t_Kdb = b.tiles("Kdb", 2)
    Qdb = [ph.sb(f"Qdb{i}", [128, 2, 128], BF16) for i in range(2)]; t_Qdb = b.tiles("Qdb", 2)
    junk = ph.sb("junk", [128, 512], F32); t_junk = b.tile("junk")
    ssc = [ph.sb(f"ssc{i}", [128, 1], F32) for i in range(2)]; t_ssc = b.tiles("ssc", 2)
    on = [ph.sb(f"on{i}", [128, 512], F32) for i in range(2)]; t_on = b.tiles("on", 2)
    yb = [ph.sb(f"yb{i}", [128, 4, 128], BF16) for i in range(2)]; t_yb = b.tiles("yb", 2)
    GRP = split(LP, 384)
    NG = len(GRP)
    assert GRP[-1][1] >= 256 or NG == 1
    xcnt = [0]

    def rotary(h):
        hs = h % 2
        for (zsrc, Xb, t_xb) in ((dr['zq'], Qb[hs], t_qb[hs]), (dr['zk'], Kb[hs], t_kb[hs])):
            for (a0, an) in split(LP, RC):
                j = xcnt[0] % 2
                xcnt[0] += 1
                lo = max(a0, 112)
                for i in range(2):
                    if a0 < 112:
                        b.op('pool', lambda e, j=j, i=i, a0=a0: e.memset(XC[j][i][:, 0:112 - a0], 0.0), writes=[t_xc[j][i]])
                    b.dma('pool', f'ret_x{j}{i}', XC[j][i][:, lo - a0:an],
                          zsrc[h * 256 + i * 128:h * 256 + (i + 1) * 128, lo - 112:a0 + an - 112], writes=[t_xc[j][i]])
                X0, X1 = XC[j]
                tx0, tx1 = t_xc[j]
                b.op('pool', lambda e, a0=a0, an=an, X1=X1: e.tensor_tensor(TA[:, 0:an], X1[:, 0:an], sinT[:, a0:a0 + an], ALU.mult), reads=[tx1, c['t']], writes=[t_ta])
                b.op('pool', lambda e, a0=a0, an=an, X0=X0: e.tensor_tensor(TB_[:, 0:an], X0[:, 0:an], cosT[:, a0:a0 + an], ALU.mult), reads=[tx0, c['t']], writes=[t_tb])
                b.op('pool', lambda e, a0=a0, an=an, Xb=Xb: e.tensor_tensor(Xb[:, 0, a0:a0 + an], TB_[:, 0:an], TA[:, 0:an], ALU.subtract), reads=[t_ta, t_tb], writes=[t_xb])
                b.op('pool', lambda e, a0=a0, an=an, X0=X0: e.tensor_tensor(TA[:, 0:an], X0[:, 0:an], sinT[:, a0:a0 + an], ALU.mult), reads=[tx0, c['t']], writes=[t_ta])
                b.op('pool', lambda e, a0=a0, an=an, X1=X1: e.tensor_tensor(TB_[:, 0:an], X1[:, 0:an], cosT[:, a0:a0 + an], ALU.mult), reads=[tx1, c['t']], writes=[t_tb])
                b.op('pool', lambda e, a0=a0, an=an, Xb=Xb: e.tensor_tensor(Xb[:, 1, a0:a0 + an], TB_[:, 0:an], TA[:, 0:an], ALU.add), reads=[t_ta, t_tb], writes=[t_xb])

    def load_group(h, gi):
        g0, gn = GRP[gi]
        vi = (h * NG + gi) % 2
        vf, tvf, gf, tgf = VF[vi], t_vf[vi], GF[vi], t_gf[vi]
        lo = max(g0, 112)
        if g0 < 112:
            b.op('dve', lambda e, vf=vf: e.memset(vf[:, :, 0:112], 0.0), writes=[tvf])
            b.op('dve', lambda e, gf=gf: e.memset(gf[:, :, 0:112], 0.0), writes=[tgf])
        b.dma('sp', f'ret_v{vi}', vf[:, :, lo - g0:gn],
              dr['zv'][h * 512:(h + 1) * 512, lo - 112:g0 + gn - 112].rearrange("(i p) t -> p i t", p=128), writes=[tvf])
        b.dma('sp', f'ret_g{vi}', gf[:, :, lo - g0:gn],
              dr['zg'][h * 512:(h + 1) * 512, lo - 112:g0 + gn - 112].rearrange("(i p) t -> p i t", p=128), writes=[tgf])
        b.op('act', lambda e, gf=gf, gn=gn: e.activation(gf[:, :, 0:gn], gf[:, :, 0:gn], AF.Silu), reads=[tgf], writes=[tgf])
        b.op('dve', lambda e, gf=gf, gn=gn, h=h: e.tensor_tensor(gf[:, :, 0:gn], gf[:, :, 0:gn],
                                                           rg[:, h * 4:h * 4 + 4].unsqueeze(2).broadcast_to([128, 4, gn]), ALU.mult),
             reads=[tgf, c['t']], writes=[tgf])

    def front_a(h, r, k):
        hs = h % 2
        p = k % 2
        gi, go = divmod(r, 3)
        vi = (h * NG + gi) % 2
        vf, tvf = VF[vi], t_vf[vi]
        c0 = r * 128
        o0 = go * 128
        ps, tp = ph.bank()
        for i in range(2):
            b.mm(ps[:, 0:128], Kb[hs][:, i, c0:c0 + 128], Qb[hs][:, i, c0:c0 + 128], i == 0, i == 1, reads=[t_kb[hs], t_qb[hs]], writes=[tp])
        b.op('dve', lambda e, ps=ps, h=h, p=p: e.tensor_tensor(sTb[p][:, :], ps[:, 0:128], Dm[:, h, :], ALU.mult), reads=[tp, c['t']], writes=[t_sTb[p]])
        ps, tp = ph.bank()
        for i in range(4):
            b.op('pe', lambda e, ps=ps, i=i, vf=vf, o0=o0: e.transpose(ps[:, i * 128:(i + 1) * 128], vf[:, i, o0:o0 + 128], c['ident_f'][:, :]),
                 reads=[tvf, c['t']], writes=[tp])
        b.op('act', lambda e, ps=ps, p=p: e.activation(Vb[p][:, :], ps[:, :], AF.Copy), reads=[tp], writes=[t_Vb[p]])
        ps, tp = ph.bank()
        psb = ps[:, 0:128].bitcast(BF16)
        for i in range(2):
            b.op('pe', lambda e, psb=psb, i=i, c0=c0, hs=hs: e.transpose(psb[:, i * 128:(i + 1) * 128], Kb[hs][:, i, c0:c0 + 128], c['ident_b'][:, :]),
                 reads=[t_kb[hs], c['t']], writes=[tp])
        b.op('dve', lambda e, psb=psb, h=h, p=p: e.tensor_scalar_mul(Kdb[p][:, :], psb[:, 0:256], kd[:, h:h + 1]), reads=[tp, c['t']], writes=[t_Kdb[p]])
        b.op('dve', lambda e, c0=c0, h=h, hs=hs, p=p: e.tensor_tensor(Qdb[p][:, :, :], Qb[hs][:, :, c0:c0 + 128],
                                                                 tq[:, h, :].unsqueeze(1).broadcast_to([128, 2, 128]), ALU.mult),
             reads=[t_qb[hs], c['t']], writes=[t_Qdb[p]])

    def front_b(h, r, k):
        p = k % 2
        gam = 1.0 - 2.0 ** (-5.0 - h)
        cd128 = gam ** 128
        if r == 0:
            b.op('dve', lambda e: e.memset(S[:, :, :], 0.0), writes=[t_S])
            b.op('dve', lambda e: e.memset(Sb[:, :, :], 0.0), writes=[t_Sb])
        pso, tpo = PSO[p]
        b.mm(pso[:, :], sTb[p][:, :], Vb[p][:, :], True, False, reads=[t_sTb[p], t_Vb[p]], writes=[tpo])
        for i in range(2):
            b.mm(pso[:, :], Qdb[p][:, i, :], Sb[:, i, :], False, i == 1, reads=[t_Qdb[p], t_Sb], writes=[tpo])
        for i in range(2):
            b.mm(PST[:, i * 512:(i + 1) * 512], Kdb[p][:, i * 128:(i + 1) * 128], Vb[p][:, :], True, True, reads=[t_Kdb[p], t_Vb[p]], writes=[t_pst])
        b.op('dve', lambda e, cd128=cd128: e.scalar_tensor_tensor(S[:, :, :], S[:, :, :], cd128, PST[:, :].rearrange("p (i e) -> p i e", i=2), ALU.mult, ALU.add),
             reads=[t_pst, t_S], writes=[t_S])
        b.op('act', lambda e: e.activation(Sb[:, :, :], S[:, :, :], AF.Copy), reads=[t_S], writes=[t_Sb])
        b.op('act', lambda e, pso=pso, p=p: e.activation(junk[:, :], pso[:, :], AF.Square, accum_out=ssc[p][:, 0:1]), reads=[tpo], writes=[t_junk, t_ssc[p]])
        b.op('act', lambda e, p=p: e.activation(ssc[p][:, :], ssc[p][:, :], AF.Sqrt, bias=c['eps'][:, 0:1], scale=1.0 / 512), reads=[t_ssc[p], c['t']], writes=[t_ssc[p]])
        b.op('dve', lambda e, p=p: e.reciprocal(ssc[p][:, :], ssc[p][:, :]), reads=[t_ssc[p]], writes=[t_ssc[p]])
        b.op('act', lambda e, pso=pso, p=p: e.activation(on[p][:, :], pso[:, :], AF.Copy, scale=ssc[p][:, 0:1]), reads=[tpo, t_ssc[p]], writes=[t_on[p]])

    def back(h, r, k):
        p = k % 2
        gi, go = divmod(r, 3)
        vi = (h * NG + gi) % 2
        gf, tgf = GF[vi], t_gf[vi]
        c0 = r * 128
        o0 = go * 128
        ps, tp = ph.bank()
        for i in range(4):
            b.op('pe', lambda e, ps=ps, i=i, p=p: e.transpose(ps[:, i * 128:(i + 1) * 128], on[p][:, i * 128:(i + 1) * 128], c['ident_f'][:, :]),
                 reads=[t_on[p], c['t']], writes=[tp])
        y, ty = yb[p], t_yb[p]
        b.op('dve', lambda e, ps=ps, y=y, gf=gf, o0=o0: e.tensor_tensor(
            y[:, :, :], ps[:, :].rearrange("p (i t) -> p i t", i=4), gf[:, :, o0:o0 + 128], ALU.mult),
            reads=[tp, tgf], writes=[ty])
        lo = max(c0, 112)
        b.dma('sp', f'ret_y{p}',
              dr['yT1'][h * 512:(h + 1) * 512, lo - 112:c0 + 128 - 112].rearrange("(i p) t -> p i t", p=128),
              y[:, :, lo - c0:128], reads=[ty])

    seq = [(h, r) for h in range(RH) for r in range(NT)]
    rotary(0)
    if RH > 1:
        rotary(1)
    load_group(0, 0)
    front_a(seq[0][0], seq[0][1], 0)
    prev = None
    for k, (h, r) in enumerate(seq):
        if k + 1 < len(seq):
            front_a(seq[k + 1][0], seq[k + 1][1], k + 1)
        front_b(h, r, k)
        if prev is not None:
            back(prev[0], prev[1], k - 1)
        prev = (h, r)
        gi, go = divmod(r, 3)
        if go == 0:
            nh, ng = (h, gi + 1) if gi + 1 < NG else (h + 1, 0)
            if nh < RH:
                load_group(nh, ng)
        if r == NT - 1 and h + 2 < RH:
            rotary(h + 2)
    back(prev[0], prev[1], len(seq) - 1)
    ph.done()


def phase_final(nc, cfg, dr):
    ph = Phase(nc, "fin")
    b = ph.b
    D, L = cfg.D, cfg.L
    KC = D // 128
    ph.psum(8)
    c = load_consts(ph, dr)
    HI = [ph.sb(f"hi{i}", [128, KC, 128], F32) for i in range(2)]; t_hi = [b.tiles(f"hi{i}_", kparts(KC)) for i in range(2)]
    HO = [ph.sb(f"ho{i}", [128, D], F32) for i in range(2)]; t_ho = b.tiles("ho", 2)
    for ti, (s, n) in enumerate(split(cfg.SEQ, 128)):
        hi, thi, ho, tho = HI[ti % 2], t_hi[ti % 2], HO[ti % 2], t_ho[ti % 2]
        for pi_, (k0, kn) in enumerate(split(KC, 8)):
            b.dma('sp', f'fin_i{ti%2}_{pi_}', hi[:, k0:k0 + kn, 0:n],
                  dr['hT'][k0 * 128:(k0 + kn) * 128, NMETA + s:NMETA + s + n].rearrange("(kc p) t -> p kc t", p=128), writes=[thi[pi_]])
        for g in range(KC // 4):
            ps, tp = ph.bank()
            for q in range(4):
                kc = g * 4 + q
                b.op('pe', lambda e, ps=ps, q=q, hi=hi, kc=kc, n=n: e.transpose(ps[0:n, q * 128:(q + 1) * 128], hi[:, kc, 0:n], c['ident_f'][:, :]),
                     reads=[thi[kc // 8], c['t']], writes=[tp])
            if g % 2 == 0:
                b.op('act', lambda e, ps=ps, ho=ho, g=g, n=n: e.activation(ho[0:n, g * 512:(g + 1) * 512], ps[0:n, :], AF.Copy), reads=[tp], writes=[tho])
            else:
                b.op('dve', lambda e, ps=ps, ho=ho, g=g, n=n: e.tensor_copy(ho[0:n, g * 512:(g + 1) * 512], ps[0:n, :]), reads=[tp], writes=[tho])
        b.dma('sp', f'fin_o{ti%2}', dr['out'][s:s + n, :], ho[0:n, :], reads=[tho])
    ph.done()


def host_consts(cfg):
    L, LP, RH = cfg.L, cfg.LP, cfg.RH
    cst = {}
    cst['c_ident'] = np.eye(128, dtype=np.float32)
    p = np.arange(128)[:, None, None]
    mi = np.arange(4)[None, :, None]
    cc = np.arange(512)[None, None, :]
    cst['c_foxmask'] = np.where(cc - 128 * mi - p >= 0, 0.0, -30000.0).astype(np.float32)
    half = 128
    inv = (10000.0 ** (-np.arange(half, dtype=np.float32) / half)).astype(np.float32)
    pos = (np.arange(LP) - 112).astype(np.float32)
    ang = (pos[None, :] * inv[:, None]).astype(np.float32)
    cst['c_cos'] = np.cos(ang).astype(np.float32)
    cst['c_sin'] = np.sin(ang).astype(np.float32)
    lg = np.log(1.0 - 2.0 ** (-5.0 - np.arange(RH, dtype=np.float64)))
    m = np.arange(128)[:, None]
    cq = np.arange(128)[None, :]
    same = (m // 64) == (cq // 64)
    earlier = (m // 64) < (cq // 64)
    Dm = np.zeros((128, RH, 128), np.float64)
    for h in range(RH):
        d = np.where(same, np.exp(lg[h] * np.abs(cq - m)), np.where(earlier, np.exp(lg[h] * (cq - m)), 0.0))
        Dm[:, h, :] = d / 16.0
    cst['c_Dm'] = Dm.astype(np.float32)
    tq = np.exp(lg[None, :, None] * (np.arange(128)[None, None, :] + 1.0)) * np.ones((128, 1, 1))
    cst['c_tq'] = tq.astype(np.float32)
    kd = np.exp(lg[None, :] * (127.0 - np.arange(128)[:, None])) / 16.0
    cst['c_kd'] = kd.astype(np.float32)
    return cst


PARAMS = ["meta_tokens", "ab_norm", "ab_w_in", "ab_b_f", "ab_conv_w", "ab_conv_b", "ab_w_a", "ab_b_a", "ab_w_x",
          "ab_b_x", "ab_lambda", "ab_q_norm", "ab_k_norm", "ab_w_out", "c_norm", "c_w_in", "c_ret_norm", "c_w_out",
          "ffn_norm", "ffn_w_gate", "ffn_w_up", "ffn_w_down"]


def build_program(cfg, shapes, upto=99):
    nc = bass.Bass("TRN2", target_bir_lowering=False)
    dr = {}
    dr['x'] = nc.dram_tensor("x", [cfg.SEQ, cfg.D], F32, kind="ExternalInput").ap()
    for k in PARAMS:
        shp = [s for s in shapes[k]]
        if shp[0] == 1 and k != "ffn_norm" and not k.startswith("ffn_w"):
            shp = shp[1:]
        nm = 'meta' if k == 'meta_tokens' else k
        dr[nm] = nc.dram_tensor(nm, shp, F32, kind="ExternalInput").ap()
    cst = host_consts(cfg)
    for k, v in cst.items():
        dr[k] = nc.dram_tensor(k, list(v.shape), F32, kind="ExternalInput").ap()
    dr['out'] = nc.dram_tensor("out", [cfg.SEQ, cfg.D], F32, kind="ExternalOutput").ap()
    D, L = cfg.D, cfg.L
    dr['hT'] = nc.dram_tensor("hT", [D, L], F32, kind="Internal").ap()
    dr['zT0'] = nc.dram_tensor("zT0", [cfg.AB_IN, L], F32, kind="Internal").ap()
    dr['yT0'] = nc.dram_tensor("yT0", [cfg.AB_MIX, L], BF16, kind="Internal").ap()
    dr['HT'] = nc.dram_tensor("HT", [cfg.DFF, L], BF16, kind="Internal").ap()
    dr['zq'] = nc.dram_tensor("zq", [cfg.RQK, L], F32, kind="Internal").ap()
    dr['zk'] = nc.dram_tensor("zk", [cfg.RQK, L], F32, kind="Internal").ap()
    dr['zv'] = nc.dram_tensor("zv", [cfg.RV, L], F32, kind="Internal").ap()
    dr['zg'] = nc.dram_tensor("zg", [cfg.RV, L], F32, kind="Internal").ap()
    dr['yT1'] = nc.dram_tensor("yT1", [cfg.RV, L], BF16, kind="Internal").ap()
    dr['cqD'] = nc.dram_tensor("cqD", [cfg.FH, 3, L], BF16, kind="Internal").ap()

    steps = [
        lambda: phase_prep(nc, cfg, dr),
        lambda: phase_inproj(nc, cfg, dr, "in0", dr['ab_norm'], dr['ab_w_in'], cfg.AB_IN, [(0, cfg.AB_IN, dr['zT0'])]),
        lambda: phase_lru(nc, cfg, dr),
        lambda: phase_fox(nc, cfg, dr),
        lambda: phase_outproj(nc, cfg, dr, "op0", dr['yT0'], cfg.AB_MIX, dr['ab_w_out'], 1040, 512),
        lambda: phase_ffn_gu(nc, cfg, dr, "gu0", dr['ffn_norm'][0], dr['ffn_w_gate'][0], dr['ffn_w_up'][0]),
        lambda: phase_outproj(nc, cfg, dr, "dn0", dr['HT'], cfg.DFF, dr['ffn_w_down'][0], 528, 128),
        lambda: phase_inproj(nc, cfg, dr, "in1", dr['c_norm'], dr['c_w_in'], cfg.RET_IN,
                             [(0, cfg.RQK, dr['zq']), (cfg.RQK, 2 * cfg.RQK, dr['zk']),
                              (2 * cfg.RQK, 2 * cfg.RQK + cfg.RV, dr['zv']), (2 * cfg.RQK + cfg.RV, cfg.RET_IN, dr['zg'])]),
        lambda: phase_ret(nc, cfg, dr),
        lambda: phase_outproj(nc, cfg, dr, "op1", dr['yT1'], cfg.RV, dr['c_w_out'], 528, 256),
        lambda: phase_ffn_gu(nc, cfg, dr, "gu1", dr['ffn_norm'][1], dr['ffn_w_gate'][1], dr['ffn_w_up'][1]),
        lambda: phase_outproj(nc, cfg, dr, "dn1", dr['HT'], cfg.DFF, dr['ffn_w_down'][1], 528, 128),
    ]
    for i, s in enumerate(steps):
        if i < upto:
            s()
    phase_final(nc, cfg, dr)
    return nc, cst


_CACHE = {}


def kernel(**inputs):
    x = np.asarray(inputs["x"], dtype=np.float32)
    B, SEQ, D = x.shape
    cfg = Cfg(D, SEQ)
    shapes = {k: tuple(np.asarray(inputs[k]).shape) for k in PARAMS}
    key = (D, SEQ)
    if key not in _CACHE:
        _CACHE[key] = build_program(cfg, shapes)
    nc, cst = _CACHE[key]
    shared = {}
    for k in PARAMS:
        a = np.ascontiguousarray(np.asarray(inputs[k], dtype=np.float32))
        if a.shape[0] == 1 and k != "ffn_norm" and not k.startswith("ffn_w"):
            a = a[0]
        shared['meta' if k == 'meta_tokens' else k] = a
    shared.update(cst)
    ncores = 8
    hot = list(range(B))
    in_maps = []
    for cidx in range(ncores):
        m = dict(shared)
        m['x'] = np.ascontiguousarray(x[cidx % B])
        in_maps.append(m)
    res = run_bass_kernel_spmd(nc, in_maps, core_ids=list(range(ncores)))
    out = np.stack([np.asarray(res.results[hot[bi]]["out"], dtype=np.float32) for bi in range(B)], axis=0)
    return out
```

```python
import contextlib
import numpy as np
import concourse.bass as bass
import concourse.mybir as mybir
from concourse.bass_utils import run_bass_kernel_spmd


F32 = mybir.dt.float32
BF16 = mybir.dt.bfloat16
AF = mybir.ActivationFunctionType
ALU = mybir.AluOpType
AX = mybir.AxisListType

ENGS = ['pe', 'act', 'dve', 'pool', 'sp']
EPOCH = 8000


class Tile:
    __slots__ = ('name', 'last_w', 'readers')

    def __init__(self, name):
        self.name = name
        self.last_w = None
        self.readers = {}


class Op:
    __slots__ = ('eng', 'fn', 'deps', 'chan', 'flag', 'val', 'idx')

    def __init__(self, eng, fn, deps, chan):
        self.eng = eng
        self.fn = fn
        self.deps = deps
        self.chan = chan
        self.flag = chan is not None
        self.val = 0


class Builder:
    _uid = 0

    def __init__(self, nc):
        self.nc = nc
        self.ops = {e: [] for e in ENGS}
        self.chan_last = {}
        self.chan_eng = {}
        self.nops = 0

    def tile(self, name):
        return Tile(name)

    def tiles(self, name, n):
        return [Tile(f"{name}{i}") for i in range(n)]

    def op(self, eng, fn, reads=(), writes=(), chan=None):
        deps = {}
        for t in reads:
            if t.last_w is not None:
                deps[id(t.last_w)] = t.last_w
        for t in writes:
            if t.last_w is not None:
                deps[id(t.last_w)] = t.last_w
            for r in t.readers.values():
                deps[id(r)] = r
        if chan is not None:
            assert self.chan_eng.setdefault(chan, eng) == eng, chan
            prev = self.chan_last.get(chan)
            if prev is not None:
                deps[id(prev)] = prev
        o = Op(eng, fn, list(deps.values()), chan)
        if chan is not None:
            self.chan_last[chan] = o
        key = ('c', chan) if chan is not None else ('e', eng)
        for t in reads:
            t.readers[key] = o
        for t in writes:
            t.last_w = o
            t.readers = {}
        self.ops[eng].append(o)
        self.nops += 1
        return o

    def dma(self, eng, chan, out, in_, reads=(), writes=(), **kw):
        return self.op(eng, lambda e: e.dma_start(out=out, in_=in_, **kw), reads, writes, chan=chan)

    def mm(self, out, lhsT, rhs, start, stop, reads=(), writes=()):
        return self.op('pe', lambda e: e.matmul(out, lhsT, rhs, start=start, stop=stop), reads, writes)

    def emit(self):
        nc = self.nc
        for e in ENGS:
            for o in self.ops[e]:
                for d in o.deps:
                    if d.chan is None and d.eng == 'pe' and o.eng == 'pe' and o.chan is None:
                        continue
                    d.flag = True
        totals = {}
        for e in ENGS:
            v = 0
            cv = {}
            for o in self.ops[e]:
                if o.chan is not None:
                    cv[o.chan] = cv.get(o.chan, 0) + 1
                    o.val = cv[o.chan]
                elif o.flag:
                    v += 1
                    o.val = v
            totals[('e', e)] = v
            for c, n in cv.items():
                totals[('c', c)] = n
        sems = {}
        for key, n in totals.items():
            for ep in range((n + EPOCH - 1) // EPOCH):
                Builder._uid += 1
                nm = f"s{Builder._uid}_{key[0]}_{key[1]}_{ep}"
                sems[(key, ep)] = nc.alloc_semaphore(name=nm)
        self.n_sems = len(sems)
        with nc.Block() as block:

            def run(ename):
                def body(eh):
                    known = {}
                    for o in self.ops[ename]:
                        for d in o.deps:
                            if d.chan is None and d.eng == 'pe' and ename == 'pe' and o.chan is None:
                                continue
                            key = ('c', d.chan) if d.chan is not None else ('e', d.eng)
                            ep = (d.val - 1) // EPOCH
                            v = (d.val - 1) % EPOCH + 1
                            mult = 16 if d.chan is not None else 1
                            kk = (key, ep)
                            if any(k2[0] == key and k2[1] > ep for k2 in known):
                                continue
                            if known.get(kk, 0) >= v:
                                continue
                            eh.wait_ge(sems[kk], v * mult)
                            known[kk] = v
                        ins = o.fn(eh)
                        if o.flag and ins is not None:
                            if o.chan is not None:
                                key = ('c', o.chan)
                                ins.then_inc(sems[(key, (o.val - 1) // EPOCH)], 16)
                            else:
                                key = ('e', ename)
                                ins.then_inc(sems[(key, (o.val - 1) // EPOCH)], 1)
                    if ename == 'sp':
                        for ch, last in self.chan_last.items():
                            key = ('c', ch)
                            eh.wait_ge(sems[(key, (last.val - 1) // EPOCH)], ((last.val - 1) % EPOCH + 1) * 16)
                return body

            block.tensor(run('pe'))
            block.scalar(run('act'))
            block.vector(run('dve'))
            block.gpsimd(run('pool'))
            block.sync(run('sp'))
        nc.clear_and_free_semaphores(list(sems.values()))
        nc.all_engine_barrier()

    def final_wait(self, eng, ops):
        deps = list(ops)
        o = Op(eng, lambda e: None, deps, None)
        self.ops[eng].append(o)
        return o

import math
import numpy as np
import ml_dtypes

NMETA = 16
EPS = 1e-6


class Cfg:
    def __init__(self, D=4096, SEQ=4096):
        self.D = D
        self.SEQ = SEQ
        self.L = SEQ + NMETA
        self.LW = D // 2
        self.NLB = self.LW // 128
        self.FH = (D // 2) // 128
        self.FW = self.FH * 128
        self.AB_IN = 2 * self.LW + 3 * self.FW + self.FH
        self.AB_MIX = self.LW + self.FW
        self.RH = D // 256
        self.RQK = self.RH * 256
        self.RV = self.RH * 512
        self.RET_IN = 2 * self.RQK + 2 * self.RV
        self.DFF = -(-8 * D // (3 * 256)) * 256
        self.LP = 112 + self.L
        self.NT = self.LP // 128


def split(n, m):
    k = -(-n // m)
    out = []
    s = 0
    for i in range(k):
        e = min(n, s + m)
        out.append((s, e - s))
        s = e
    return out


def tok_blocks(L, maxblk):
    blks = []
    first = min(L, NMETA + (maxblk - NMETA) // 512 * 512) if maxblk >= 512 + NMETA else min(L, maxblk)
    blks.append((0, first))
    s = first
    step = maxblk // 512 * 512 if maxblk >= 512 else maxblk
    while s < L:
        n = min(step, L - s)
        blks.append((s, n))
        s += n
    return blks


def blk_tiles(n):
    if n % 512 == NMETA:
        return split(n - NMETA, 512) + [(n - NMETA, NMETA)] if n > NMETA else [(0, n)]
    return split(n, 512)


class Phase:
    def __init__(self, nc, name):
        self.nc = nc
        self.name = name
        self.st = contextlib.ExitStack()
        self.b = Builder(nc)
        self.ps = []
        self.t_ps = []
        self.pi = 0

    def sb(self, name, shape, dt):
        return self.st.enter_context(self.nc.sbuf_tensor(f"{self.name}_{name}", list(shape), dt))

    def psum(self, n=8):
        for i in range(n):
            self.ps.append(self.st.enter_context(self.nc.psum_tensor(f"{self.name}_ps{i}", [128, 512], F32)))
            self.t_ps.append(self.b.tile(f"ps{i}"))

    def bank(self):
        i = self.pi % len(self.ps)
        self.pi += 1
        return self.ps[i], self.t_ps[i]

    def done(self):
        self.b.emit()
        self.st.close()


def load_consts(ph, dr):
    b = ph.b
    c = {}
    c['ident_f'] = ph.sb("identf", [128, 128], F32)
    c['ident_b'] = ph.sb("identb", [128, 128], BF16)
    c['ones_b'] = ph.sb("onesb", [128, 128], BF16)
    c['t'] = b.tile("consts")
    b.dma('sp', 'cst', c['ident_f'][:, :], dr['c_ident'], writes=[c['t']])
    b.dma('pool', 'cstp', c['ident_b'][:, :], dr['c_ident'], writes=[c['t']])
    b.op('dve', lambda e: e.memset(c['ones_b'][:, :], 1.0), writes=[c['t']])
    return c


def gemm(ph, XT, t_xt, KC, tiles, W, ncols, epi, WB, t_wb, wcols, wstate, groups=1, W2=None, WB2=None, t_wb2=None):
    b = ph.b
    chunks = split(ncols, wcols)
    srcs = [(W, WB, t_wb)] + ([(W2, WB2, t_wb2)] if W2 is not None else [])

    def wload(ci):
        c0, wc = chunks[ci]
        s = (wstate[0] + ci) % 2
        for gi, (Wd, WBd, twd) in enumerate(srcs):
            for pi_, (k0, kn) in enumerate(split(KC, 8)):
                b.dma('pool', f'{ph.name}_w{gi}{s}_{pi_}', WBd[s][:, k0:k0 + kn, 0:wc],
                      Wd[k0 * 128:(k0 + kn) * 128, c0:c0 + wc].rearrange("(kc p) n -> p kc n", p=128), writes=[twd[s][pi_]])

    wload(0)
    for ci, (c0, wc) in enumerate(chunks):
        if ci + 1 < len(chunks):
            wload(ci + 1)
        s = (wstate[0] + ci) % 2
        for cb0, m in split(wc, 128):
            for ti, (t0, n) in enumerate(tiles):
                banks = []
                for gi, (Wd, WBd, twd) in enumerate(srcs):
                    ps, tp = ph.bank()
                    for kc in range(KC):
                        b.mm(ps[0:m, 0:n], WBd[s][:, kc, cb0:cb0 + m], XT[:, kc, t0:t0 + n],
                             start=(kc == 0), stop=(kc == KC - 1), reads=[t_xt[kc // 8], twd[s][kc // 8]], writes=[tp])
                    banks += [ps, tp]
                epi(c0 + cb0, m, ti, t0, n, *banks)
    wstate[0] = (wstate[0] + len(chunks)) % 2


def kparts(KC):
    return len(split(KC, 8))


def norm_prologue(ph, c, hT, gain, tok0, ntok, XT, t_xt, KC, D, ST, t_st, SQ, t_sq, RSB, t_rsb, cnt):
    b = ph.b
    for (s0, sn) in split(ntok, 128):
        i = cnt[0] % 2
        cnt[0] += 1
        st, tst = ST[i], t_st[i]
        for pi_, (k0, kn) in enumerate(split(KC, 8)):
            b.dma('sp', f'{ph.name}_st{i}_{pi_}', st[:, k0:k0 + kn, 0:sn],
                  hT[k0 * 128:(k0 + kn) * 128, tok0 + s0:tok0 + s0 + sn].rearrange("(kc p) t -> p kc t", p=128), writes=[tst[pi_]])
        b.op('act', lambda e, st=st, sn=sn: e.activation(SQ[:, :, 0:sn], st[:, :, 0:sn], AF.Square),
             reads=tst, writes=[t_sq])
        ps, tp = ph.bank()
        for kc in range(KC):
            b.mm(ps[:, 0:sn], c['ones_b'][:, :], SQ[:, kc, 0:sn], start=(kc == 0), stop=(kc == KC - 1),
                 reads=[t_sq, c['t']], writes=[tp])
        b.op('act', lambda e, ps=ps, s0=s0, sn=sn: e.activation(RSB[:, s0:s0 + sn], ps[:, 0:sn], AF.Sqrt, bias=c['eps'][:, 0:1], scale=1.0 / D),
             reads=[tp, c['t']], writes=[t_rsb])
        b.op('dve', lambda e, s0=s0, sn=sn: e.reciprocal(RSB[:, s0:s0 + sn], RSB[:, s0:s0 + sn]), reads=[t_rsb], writes=[t_rsb])
        b.op('dve', lambda e, st=st, s0=s0, sn=sn: e.tensor_tensor(
            XT[:, :, s0:s0 + sn], st[:, :, 0:sn], gain[:, :].unsqueeze(2).broadcast_to([128, KC, sn]), ALU.mult),
            reads=list(tst) + [c['t']], writes=list(t_xt))


def load_cols(ph, c, name, vec, K):
    t = ph.sb(name, [128, K // 128], F32)
    ph.b.dma('sp', 'cst', t[:, :], vec.rearrange("(k p) -> p k", p=128), writes=[c['t']], allow_slow_non_contiguous=True)
    return t


def make_eps(ph, c):
    c['eps'] = ph.sb("eps", [128, 1], F32)
    ph.b.op('dve', lambda e: e.memset(c['eps'][:, :], EPS), writes=[c['t']])


def phase_prep(nc, cfg, dr):
    ph = Phase(nc, "p0")
    b = ph.b
    D, L = cfg.D, cfg.L
    KC = D // 128
    ph.psum(8)
    c = load_consts(ph, dr)
    XI = [ph.sb(f"xi{i}", [128, D], F32) for i in range(2)]
    t_xi = b.tiles("xi", 2)
    XO = [ph.sb(f"xo{i}", [128, KC, 128], F32) for i in range(2)]
    t_xo = b.tiles("xo", 2)
    ttiles = [('m', 0, NMETA)] + [('x', s, n) for (s, n) in split(cfg.SEQ, 128)]
    for i, (kind, s, n) in enumerate(ttiles):
        xi, txi, xo, txo = XI[i % 2], t_xi[i % 2], XO[i % 2], t_xo[i % 2]
        src = dr['meta'][0:n, :] if kind == 'm' else dr['x'][s:s + n, :]
        b.dma('pool', f'p0_xi{i%2}', xi[0:n, :], src, writes=[txi])
        for g in range(KC // 4):
            ps, tp = ph.bank()
            for q in range(4):
                kc = g * 4 + q
                b.op('pe', lambda e, ps=ps, q=q, xi=xi, kc=kc, n=n: e.transpose(
                    ps[:, q * 128:q * 128 + n], xi[0:n, kc * 128:(kc + 1) * 128], c['ident_f'][0:n, 0:n]),
                    reads=[txi, c['t']], writes=[tp])
            eng = 'act' if g % 2 == 0 else 'dve'
            if eng == 'act':
                b.op('act', lambda e, ps=ps, xo=xo, g=g, n=n: e.activation(
                    xo[:, g * 4:g * 4 + 4, 0:n], ps[:, :].rearrange("p (q t) -> p q t", q=4)[:, :, 0:n], AF.Copy),
                    reads=[tp], writes=[txo])
            else:
                b.op('dve', lambda e, ps=ps, xo=xo, g=g, n=n: e.tensor_copy(
                    xo[:, g * 4:g * 4 + 4, 0:n], ps[:, :].rearrange("p (q t) -> p q t", q=4)[:, :, 0:n]),
                    reads=[tp], writes=[txo])
        tok0 = 0 if kind == 'm' else NMETA + s
        for pi_, (k0, kn) in enumerate(split(KC, 8)):
            b.dma('sp', f'p0_xo{i%2}_{pi_}', dr['hT'][k0 * 128:(k0 + kn) * 128, tok0:tok0 + n].rearrange("(kc p) t -> p kc t", p=128),
                  xo[:, k0:k0 + kn, 0:n], reads=[txo])
    ph.done()


def phase_inproj(nc, cfg, dr, name, gain_vec, W, ncols, zT):
    ph = Phase(nc, name)
    b = ph.b
    D, L = cfg.D, cfg.L
    KC = D // 128
    ph.psum(8)
    c = load_consts(ph, dr)
    make_eps(ph, c)
    gain = load_cols(ph, c, "gain", gain_vec, D)
    TB = 1040
    XT = ph.sb("XT", [128, KC, TB], BF16)
    t_xt = b.tiles("xt", kparts(KC))
    wcols = 512
    WB = [ph.sb(f"wb{i}", [128, KC, wcols], BF16) for i in range(2)]
    t_wb = [b.tiles(f"wb{i}_", kparts(KC)) for i in range(2)]
    ST = [ph.sb(f"st{i}", [128, KC, 128], F32) for i in range(2)]
    t_st = [b.tiles(f"st{i}_", kparts(KC)) for i in range(2)]
    SQ = ph.sb("sq", [128, KC, 128], BF16)
    t_sq = b.tile("sq")
    RSB = ph.sb("rsb", [128, TB], F32)
    t_rsb = b.tile("rsb")
    OB = [ph.sb(f"ob{i}", [128, 512], F32) for i in range(4)]
    t_ob = b.tiles("ob", 4)
    cnt = [0]
    ocnt = [0]
    wstate = [0]
    for (tok0, ntok) in tok_blocks(L, TB):
        norm_prologue(ph, c, dr['hT'], gain, tok0, ntok, XT, t_xt, KC, D, ST, t_st, SQ, t_sq, RSB, t_rsb, cnt)

        def epi(c0, m, ti, t0, n, ps, tp, tok0=tok0):
            i = ocnt[0] % 4
            ocnt[0] += 1
            ob, tob = OB[i], t_ob[i]
            b.op('dve', lambda e: e.tensor_tensor(ob[0:m, 0:n], ps[0:m, 0:n], RSB[0:m, t0:t0 + n], ALU.mult),
                 reads=[tp, t_rsb], writes=[tob])
            for (r0, r1, zap) in zT:
                if r0 <= c0 < r1:
                    b.dma('sp', f'{name}_o{i}', zap[c0 - r0:c0 - r0 + m, tok0 + t0:tok0 + t0 + n], ob[0:m, 0:n], reads=[tob])

        gemm(ph, XT, t_xt, KC, blk_tiles(ntok), W, ncols, epi, WB, t_wb, wcols, wstate)
    ph.done()


def phase_outproj(nc, cfg, dr, name, yT, K, W, TB, wcols):
    ph = Phase(nc, name)
    b = ph.b
    D, L = cfg.D, cfg.L
    KC = K // 128
    ph.psum(8)
    XT = ph.sb("XT", [128, KC, TB], BF16)
    t_xt = b.tiles("xt", kparts(KC))
    WB = [ph.sb(f"wb{i}", [128, KC, wcols], BF16) for i in range(2)]
    t_wb = [b.tiles(f"wb{i}_", kparts(KC)) for i in range(2)]
    RB = [ph.sb(f"rb{i}", [128, 512], F32) for i in range(4)]
    t_rb = b.tiles("rb", 4)
    ocnt = [0]
    wstate = [0]
    for (tok0, ntok) in tok_blocks(L, TB):
        for pi_, (k0, kn) in enumerate(split(KC, 8)):
            b.dma('sp', f'{name}_x{pi_}', XT[:, k0:k0 + kn, 0:ntok],
                  yT[k0 * 128:(k0 + kn) * 128, tok0:tok0 + ntok].rearrange("(kc p) t -> p kc t", p=128), writes=[t_xt[pi_]])

        def epi(c0, m, ti, t0, n, ps, tp, tok0=tok0):
            i = ocnt[0] % 4
            ocnt[0] += 1
            rb, trb = RB[i], t_rb[i]
            dst = dr['hT'][c0:c0 + m, tok0 + t0:tok0 + t0 + n]
            b.dma('sp', f'{name}_r{i}', rb[0:m, 0:n], dst, writes=[trb])
            b.op('dve', lambda e: e.tensor_tensor(rb[0:m, 0:n], ps[0:m, 0:n], rb[0:m, 0:n], ALU.add),
                 reads=[tp, trb], writes=[trb])
            b.dma('sp', f'{name}_r{i}', dst, rb[0:m, 0:n], reads=[trb])

        gemm(ph, XT, t_xt, KC, blk_tiles(ntok), W, D, epi, WB, t_wb, wcols, wstate)
    ph.done()


def phase_ffn_gu(nc, cfg, dr, name, gain_vec, Wg, Wu):
    ph = Phase(nc, name)
    b = ph.b
    D, L = cfg.D, cfg.L
    KC = D // 128
    ph.psum(8)
    c = load_consts(ph, dr)
    make_eps(ph, c)
    gain = load_cols(ph, c, "gain", gain_vec, D)
    TB = 1040
    XT = ph.sb("XT", [128, KC, TB], BF16)
    t_xt = b.tiles("xt", kparts(KC))
    wcols = 256
    WB = [ph.sb(f"wg{i}", [128, KC, wcols], BF16) for i in range(2)]
    t_wb = [b.tiles(f"wg{i}_", kparts(KC)) for i in range(2)]
    WB2 = [ph.sb(f"wu{i}", [128, KC, wcols], BF16) for i in range(2)]
    t_wb2 = [b.tiles(f"wu{i}_", kparts(KC)) for i in range(2)]
    ST = [ph.sb(f"st{i}", [128, KC, 128], F32) for i in range(2)]
    t_st = [b.tiles(f"st{i}_", kparts(KC)) for i in range(2)]
    SQ = ph.sb("sq", [128, KC, 128], BF16)
    t_sq = b.tile("sq")
    RSB = ph.sb("rsb", [128, TB], F32)
    t_rsb = b.tile("rsb")
    SG = [ph.sb(f"sg{i}", [128, 512], F32) for i in range(2)]
    t_sg = b.tiles("sg", 2)
    HB = [ph.sb(f"hb{i}", [128, 512], BF16) for i in range(4)]
    t_hb = b.tiles("hb", 4)
    cnt = [0]
    ocnt = [0]
    wstate = [0]
    for (tok0, ntok) in tok_blocks(L, TB):
        norm_prologue(ph, c, dr['hT'], gain, tok0, ntok, XT, t_xt, KC, D, ST, t_st, SQ, t_sq, RSB, t_rsb, cnt)

        def epi(c0, m, ti, t0, n, psg, tpg, psu, tpu, tok0=tok0):
            i = ocnt[0] % 4
            ocnt[0] += 1
            sg, tsg = SG[i % 2], t_sg[i % 2]
            hb, thb = HB[i], t_hb[i]
            b.op('dve', lambda e: e.tensor_tensor(sg[0:m, 0:n], psg[0:m, 0:n], RSB[0:m, t0:t0 + n], ALU.mult),
                 reads=[tpg, t_rsb], writes=[tsg])
            b.op('act', lambda e: e.activation(sg[0:m, 0:n], sg[0:m, 0:n], AF.Silu), reads=[tsg], writes=[tsg])
            b.op('dve', lambda e: e.tensor_tensor(sg[0:m, 0:n], sg[0:m, 0:n], RSB[0:m, t0:t0 + n], ALU.mult),
                 reads=[tsg, t_rsb], writes=[tsg])
            b.op('dve', lambda e: e.tensor_tensor(hb[0:m, 0:n], psu[0:m, 0:n], sg[0:m, 0:n], ALU.mult),
                 reads=[tpu, tsg], writes=[thb])
            b.dma('sp', f'{name}_o{i}', dr['HT'][c0:c0 + m, tok0 + t0:tok0 + t0 + n], hb[0:m, 0:n], reads=[thb])

        gemm(ph, XT, t_xt, KC, blk_tiles(ntok), Wg, cfg.DFF, epi, WB, t_wb, wcols, wstate,
             W2=Wu, WB2=WB2, t_wb2=t_wb2)
    ph.done()


def phase_lru(nc, cfg, dr):
    ph = Phase(nc, "lru")
    b = ph.b
    L, LW, NLB = cfg.L, cfg.LW, cfg.NLB
    ph.psum(4)
    c = load_consts(ph, dr)
    zT = dr['zT0']
    cw = ph.sb("cw", [128, 4, NLB], F32)
    b.dma('sp', 'cst', cw[:, :, :], dr['ab_conv_w'].rearrange("j (n p) -> p j n", p=128), writes=[c['t']], allow_slow_non_contiguous=True)
    cb = load_cols(ph, c, "cb", dr['ab_conv_b'], LW)
    ba = load_cols(ph, c, "ba", dr['ab_b_a'], LW)
    bx = load_cols(ph, c, "bx", dr['ab_b_x'], LW)
    lam = load_cols(ph, c, "lam", dr['ab_lambda'], LW)
    cl = ph.sb("cl", [128, NLB], F32)
    cl2 = ph.sb("cl2", [128, NLB], F32)
    b.op('act', lambda e: e.activation(cl[:, :], lam[:, :], AF.Exp, scale=-1.0), reads=[c['t']], writes=[c['t']])
    b.op('dve', lambda e: e.tensor_scalar_add(cl[:, :], cl[:, :], 1.0), reads=[c['t']], writes=[c['t']])
    b.op('act', lambda e: e.activation(cl[:, :], cl[:, :], AF.Ln), reads=[c['t']], writes=[c['t']])
    b.op('dve', lambda e: e.tensor_scalar_mul(cl2[:, :], cl[:, :], -16.0), reads=[c['t']], writes=[c['t']])
    b.op('dve', lambda e: e.tensor_scalar_mul(cl[:, :], cl[:, :], -8.0), reads=[c['t']], writes=[c['t']])
    Wa = ph.sb("Wa", [128, NLB, 128], BF16)
    Wx = ph.sb("Wx", [128, NLB, 128], BF16)
    b.dma('pool', 'cstp', Wa[:, :, :], dr['ab_w_a'].rearrange("n c d -> c n d"), writes=[c['t']])
    b.dma('pool', 'cstp', Wx[:, :, :], dr['ab_w_x'].rearrange("n c d -> c n d"), writes=[c['t']])

    xp = ph.sb("xp", [128, 3 + L], F32); t_xp = b.tile("xp")
    gt = ph.sb("gt", [128, L], F32); t_gt = b.tile("gt")
    xc = ph.sb("xc", [128, L], F32); t_xc = b.tile("xc")
    xcb = ph.sb("xcb", [128, L], BF16); t_xcb = b.tile("xcb")
    rr = ph.sb("rr", [128, L], F32); t_rr = b.tile("rr")
    ii = ph.sb("ii", [128, L], F32); t_ii = b.tile("ii")
    t1 = ph.sb("t1", [128, L], F32); t_t1 = b.tile("t1")
    yb = ph.sb("yb", [128, L], BF16); t_yb = b.tile("yb")
    b.op('dve', lambda e: e.memset(xp[:, 0:3], 0.0), writes=[t_xp])
    for n in range(NLB):
        b.dma('sp', 'lru_x', xp[:, 3:3 + L], zT[n * 128:(n + 1) * 128, :], writes=[t_xp])
        b.dma('sp', 'lru_g', gt[:, :], zT[LW + n * 128:LW + (n + 1) * 128, :], writes=[t_gt])
        b.op('dve', lambda e, n=n: e.tensor_scalar(xc[:, :], xp[:, 3:3 + L], cw[:, 3, n:n + 1], cb[:, n:n + 1], ALU.mult, ALU.add),
             reads=[t_xp, c['t']], writes=[t_xc])
        for j in range(3):
            b.op('dve', lambda e, n=n, j=j: e.scalar_tensor_tensor(xc[:, :], xp[:, j:j + L], cw[:, j, n:n + 1], xc[:, :], ALU.mult, ALU.add),
                 reads=[t_xp, t_xc, c['t']], writes=[t_xc])
        b.op('act', lambda e: e.activation(xcb[:, :], xc[:, :], AF.Copy), reads=[t_xc], writes=[t_xcb])
        for (t0, tn) in split(L, 512):
            ps, tp = ph.bank()
            b.mm(ps[:, 0:tn], Wa[:, n, :], xcb[:, t0:t0 + tn], True, True, reads=[t_xcb, c['t']], writes=[tp])
            b.op('act', lambda e, ps=ps, t0=t0, tn=tn, n=n: e.activation(rr[:, t0:t0 + tn], ps[:, 0:tn], AF.Sigmoid, bias=ba[:, n:n + 1]),
                 reads=[tp, c['t']], writes=[t_rr])
            ps, tp = ph.bank()
            b.mm(ps[:, 0:tn], Wx[:, n, :], xcb[:, t0:t0 + tn], True, True, reads=[t_xcb, c['t']], writes=[tp])
            b.op('act', lambda e, ps=ps, t0=t0, tn=tn, n=n: e.activation(ii[:, t0:t0 + tn], ps[:, 0:tn], AF.Sigmoid, bias=bx[:, n:n + 1]),
                 reads=[tp, c['t']], writes=[t_ii])
        b.op('act', lambda e, n=n: e.activation(t1[:, :], rr[:, :], AF.Exp, scale=cl2[:, n:n + 1]), reads=[t_rr, c['t']], writes=[t_t1])
        b.op('dve', lambda e: e.tensor_scalar(t1[:, :], t1[:, :], -1.0, 1.0, ALU.mult, ALU.add), reads=[t_t1], writes=[t_t1])
        b.op('act', lambda e: e.activation(t1[:, :], t1[:, :], AF.Sqrt), reads=[t_t1], writes=[t_t1])
        b.op('dve', lambda e: e.tensor_tensor(ii[:, :], ii[:, :], xc[:, :], ALU.mult), reads=[t_ii, t_xc], writes=[t_ii])
        b.op('dve', lambda e: e.tensor_tensor(ii[:, :], ii[:, :], t1[:, :], ALU.mult), reads=[t_ii, t_t1], writes=[t_ii])
        b.op('act', lambda e, n=n: e.activation(rr[:, :], rr[:, :], AF.Exp, scale=cl[:, n:n + 1]), reads=[t_rr, c['t']], writes=[t_rr])
        b.op('dve', lambda e: e.tensor_tensor_scan(t1[:, :], rr[:, :], ii[:, :], 0.0, ALU.mult, ALU.add),
             reads=[t_rr, t_ii], writes=[t_t1])
        b.op('act', lambda e: e.activation(xc[:, :], gt[:, :], AF.Square), reads=[t_gt], writes=[t_xc])
        b.op('dve', lambda e: e.tensor_scalar(xc[:, :], xc[:, :], 0.044715, 1.0, ALU.mult, ALU.add), reads=[t_xc], writes=[t_xc])
        b.op('dve', lambda e: e.tensor_tensor(xc[:, :], xc[:, :], gt[:, :], ALU.mult), reads=[t_xc, t_gt], writes=[t_xc])
        b.op('act', lambda e: e.activation(xc[:, :], xc[:, :], AF.Sigmoid, scale=1.5957691216057308), reads=[t_xc], writes=[t_xc])
        b.op('dve', lambda e: e.tensor_tensor(xc[:, :], xc[:, :], gt[:, :], ALU.mult), reads=[t_xc, t_gt], writes=[t_xc])
        b.op('dve', lambda e: e.tensor_tensor(yb[:, :], t1[:, :], xc[:, :], ALU.mult), reads=[t_t1, t_xc], writes=[t_yb])
        b.dma('sp', 'lru_y', dr['yT0'][n * 128:(n + 1) * 128, :], yb[:, :], reads=[t_yb])
    ph.done()


def phase_fox(nc, cfg, dr):
    ph = Phase(nc, "fox")
    b = ph.b
    L, LW, FH, FW = cfg.L, cfg.LW, cfg.FH, cfg.FW
    ph.psum(8)
    PS_S = list(zip(ph.ps[0:4], ph.t_ps[0:4]))
    PS_O = list(zip(ph.ps[4:6], ph.t_ps[4:6]))
    PS_D = list(zip(ph.ps[6:8], ph.t_ps[6:8]))
    scnt = [0]

    def sbank():
        i = scnt[0] % 4
        scnt[0] += 1
        return PS_S[i]

    c = load_consts(ph, dr)
    make_eps(ph, c)
    zT = dr['zT0']
    q0r = 2 * LW
    k0r = 2 * LW + FW
    v0r = 2 * LW + 2 * FW
    f0r = 2 * LW + 3 * FW
    KT = split(L, 128)
    NKT = len(KT)
    SCALE = 128 ** -0.5
    qg = ph.sb("qg", [128, 1], F32)
    kg = ph.sb("kg", [128, 1], F32)
    b.dma('sp', 'cst', qg[:, :], dr['ab_q_norm'].rearrange("(p o) -> p o", o=1), writes=[c['t']], allow_slow_non_contiguous=True)
    b.dma('sp', 'cst', kg[:, :], dr['ab_k_norm'].rearrange("(p o) -> p o", o=1), writes=[c['t']], allow_slow_non_contiguous=True)
    nbf = ph.sb("nbf", [FH, 1], F32)
    b.dma('sp', 'cst', nbf[:, :], dr['ab_b_f'].rearrange("(p o) -> p o", o=1), writes=[c['t']], allow_slow_non_contiguous=True)
    b.op('dve', lambda e: e.tensor_scalar_mul(nbf[:, :], nbf[:, :], -1.0), reads=[c['t']], writes=[c['t']])
    maskb = ph.sb("maskb", [128, 4, 512], BF16)
    b.dma('pool', 'cstp', maskb[:, :, :], dr['c_foxmask'], writes=[c['t']])
    onesK = ph.sb("onesK", [128, 128], BF16)
    b.op('dve', lambda e: e.memset(onesK[:, :], 0.0), writes=[c['t']])
    b.op('dve', lambda e: e.memset(onesK[0:3, :], 1.0), writes=[c['t']])

    QF = [ph.sb(f"qf{i}", [128, L], F32) for i in range(2)]; t_QF = b.tiles("qf", 2)
    KF = [ph.sb(f"kf{i}", [128, L], F32) for i in range(2)]; t_KF = b.tiles("kf", 2)
    VFF = [ph.sb(f"vf{i}", [128, L], F32) for i in range(2)]; t_VFF = b.tiles("vf", 2)
    SQB = [ph.sb(f"sqb{i}", [128, 512], BF16) for i in range(2)]; t_SQB = b.tiles("sqb", 2)
    QN = [ph.sb(f"qn{i}", [128, L], BF16) for i in range(2)]; t_QN = b.tiles("qn", 2)
    KN = [ph.sb(f"kn{i}", [128, L], BF16) for i in range(2)]; t_KN = b.tiles("kn", 2)
    VT = [ph.sb(f"Vt{i}", [128, NKT, 128], BF16) for i in range(2)]; t_VT = b.tiles("vt", 2)
    negck = ph.sb("negck", [128, NKT, FH], F32); t_nck = b.tile("negck")
    CQ3 = [ph.sb(f"cq3{i}", [128, L], BF16) for i in range(2)]; t_CQ3 = b.tiles("cq3", 2)
    RSS = [ph.sb(f"rs{i}", [128, 512], F32) for i in range(2)]; t_RSS = b.tiles("rs", 2)
    PT = [ph.sb(f"pt{i}", [128, 512], BF16) for i in range(3)]; t_pt = b.tiles("pt", 3)
    REC = ph.sb("rec", [128, 512], F32); t_rec = b.tile("rec")
    OB = [ph.sb(f"ob{i}", [128, 512], BF16) for i in range(2)]; t_ob = b.tiles("ob", 2)

    fl, cum, one, sqb = QF[0], KF[0], VFF[0], QN[0]
    t_fl, t_cum, t_one, t_sqb = t_QF[0], t_KF[0], t_VFF[0], t_QN[0]
    b.dma('sp', 'fox_q0', fl[0:FH, :], zT[f0r:f0r + FH, :], writes=[t_fl])
    b.op('act', lambda e: e.activation(fl[0:FH, :], fl[0:FH, :], AF.Exp, bias=nbf[:, 0:1], scale=-1.0), reads=[t_fl, c['t']], writes=[t_fl])
    b.op('dve', lambda e: e.tensor_scalar_add(fl[0:FH, :], fl[0:FH, :], 1.0), reads=[t_fl], writes=[t_fl])
    b.op('act', lambda e: e.activation(fl[0:FH, :], fl[0:FH, :], AF.Ln), reads=[t_fl], writes=[t_fl])
    b.op('dve', lambda e: e.tensor_scalar_mul(fl[0:FH, :], fl[0:FH, :], -1.0), reads=[t_fl], writes=[t_fl])
    b.op('dve', lambda e: e.memset(one[0:FH, :], 1.0), writes=[t_one])
    b.op('dve', lambda e: e.tensor_tensor_scan(cum[0:FH, :], one[0:FH, :], fl[0:FH, :], 0.0, ALU.mult, ALU.add),
         reads=[t_one, t_fl], writes=[t_cum])
    for kt, (k0, kn_) in enumerate(KT):
        ps, tp = sbank()
        b.op('pe', lambda e, ps=ps, k0=k0, kn_=kn_: e.transpose(ps[0:kn_, 0:FH], cum[0:FH, k0:k0 + kn_], c['ident_f'][0:FH, 0:FH]),
             reads=[t_cum, c['t']], writes=[tp])
        b.op('dve', lambda e, ps=ps, kt=kt, kn_=kn_: e.tensor_scalar_mul(negck[0:kn_, kt, :], ps[0:kn_, 0:FH], -1.0),
             reads=[tp], writes=[t_nck])
    t_cqd = b.tile("cqD")
    b.op('dve', lambda e: e.tensor_scalar_mul(cum[0:FH, :], cum[0:FH, :], 128 ** 0.5), reads=[t_cum], writes=[t_cum])
    for part in range(3):
        b.op('dve', lambda e: e.tensor_copy(sqb[0:FH, :], cum[0:FH, :]), reads=[t_cum], writes=[t_sqb])
        b.dma('sp', 'fox_cqw', dr['cqD'][:, part, :], sqb[0:FH, :], reads=[t_sqb], writes=[t_cqd])
        if part < 2:
            b.op('dve', lambda e: e.tensor_tensor(cum[0:FH, :], cum[0:FH, :], sqb[0:FH, :], ALU.subtract),
                 reads=[t_cum, t_sqb], writes=[t_cum])
    for i in range(2):
        b.op('dve', lambda e, i=i: e.memset(CQ3[i][:, :], 0.0), writes=[t_CQ3[i]])

    QT = split(L, 512)
    ptc = [0]

    def loads(h):
        s_ = h % 2
        b.dma('sp', f'fox_q{s_}', QF[s_][:, :], zT[q0r + h * 128:q0r + (h + 1) * 128, :], writes=[t_QF[s_]])
        b.dma('sp', f'fox_k{s_}', KF[s_][:, :], zT[k0r + h * 128:k0r + (h + 1) * 128, :], writes=[t_KF[s_]])
        b.dma('sp', f'fox_v{s_}', VFF[s_][:, :], zT[v0r + h * 128:v0r + (h + 1) * 128, :], writes=[t_VFF[s_]])
        b.dma('sp', f'fox_cq{s_}', CQ3[s_][0:3, :], dr['cqD'][h, :, :], reads=[t_cqd], writes=[t_CQ3[s_]])

    def prologue_step(h, j):
        s_ = h % 2
        t0, tn = QT[j]
        for xi, (xf, t_xf, xn, t_xn, g) in enumerate(((QF[s_], t_QF[s_], QN[s_], t_QN[s_], qg), (KF[s_], t_KF[s_], KN[s_], t_KN[s_], kg))):
            sq, tsq, rs, trs = SQB[xi], t_SQB[xi], RSS[xi], t_RSS[xi]
            b.op('act', lambda e, xf=xf, sq=sq, t0=t0, tn=tn: e.activation(sq[:, 0:tn], xf[:, t0:t0 + tn], AF.Square), reads=[t_xf], writes=[tsq])
            ps, tp = sbank()
            b.mm(ps[:, 0:tn], c['ones_b'][:, :], sq[:, 0:tn], True, True, reads=[tsq, c['t']], writes=[tp])
            b.op('act', lambda e, ps=ps, rs=rs, tn=tn: e.activation(rs[:, 0:tn], ps[:, 0:tn], AF.Ln, bias=c['eps'][:, 0:1], scale=1.0 / 128),
                 reads=[tp, c['t']], writes=[trs])
            b.op('act', lambda e, rs=rs, tn=tn: e.activation(rs[:, 0:tn], rs[:, 0:tn], AF.Exp, scale=-0.5), reads=[trs], writes=[trs])
            b.op('dve', lambda e, xf=xf, xn=xn, g=g, rs=rs, t0=t0, tn=tn: e.scalar_tensor_tensor(
                xn[:, t0:t0 + tn], xf[:, t0:t0 + tn], g[:, 0:1], rs[:, 0:tn], ALU.mult, ALU.mult),
                reads=[t_xf, trs, c['t']], writes=[t_xn])
        for kt, (k0, kn_) in enumerate(KT):
            if not (t0 <= k0 < t0 + tn):
                continue
            ps, tp = sbank()
            b.op('pe', lambda e, ps=ps, k0=k0, kn_=kn_, s_=s_: e.transpose(ps[0:kn_, 0:128], VFF[s_][:, k0:k0 + kn_], c['ident_f'][:, :]),
                 reads=[t_VFF[s_], c['t']], writes=[tp])
            b.op('act', lambda e, ps=ps, kt=kt, kn_=kn_, s_=s_: e.activation(VT[s_][0:kn_, kt, :], ps[0:kn_, 0:128], AF.Copy),
                 reads=[tp], writes=[t_VT[s_]])

    def attention(h, qi):
        s_ = h % 2
        kn, qn, Vt, cq3 = KN[s_], QN[s_], VT[s_], CQ3[s_]
        t_kn, t_qn, t_vt, t_cq3 = t_KN[s_], t_QN[s_], t_VT[s_], t_CQ3[s_]
        q0, qn_ = QT[qi]
        kts = [kt for kt, (k0, kn_) in enumerate(KT) if k0 <= q0 + qn_ - 1]
        psO, tpO = PS_O[qi % 2]
        psD, tpD = PS_D[qi % 2]

        def emit_S(kt):
            k0, kn_ = KT[kt]
            ps, tp = sbank()
            diag = (k0 + kn_ - 1 > q0)
            b.mm(ps[0:kn_, 0:qn_], kn[:, k0:k0 + kn_], qn[:, q0:q0 + qn_], True, False, reads=[t_kn, t_qn], writes=[tp])
            b.mm(ps[0:kn_, 0:qn_], onesK[:, 0:kn_], cq3[:, q0:q0 + qn_], False, not diag, reads=[t_cq3, c['t']], writes=[tp])
            if diag:
                mi = (k0 - q0) // 128
                b.mm(ps[0:kn_, 0:qn_], c['ident_b'][:, 0:kn_], maskb[:, mi, 0:qn_], False, True, reads=[c['t']], writes=[tp])
            return ps, tp

        pend = [emit_S(kts[i]) for i in range(min(2, len(kts)))]
        for idx, kt in enumerate(kts):
            k0, kn_ = KT[kt]
            ps, tp = pend.pop(0)
            if idx + 2 < len(kts):
                pend.append(emit_S(kts[idx + 2]))
            pi = ptc[0] % 3
            ptc[0] += 1
            pt, tpt = PT[pi], t_pt[pi]
            b.op('act', lambda e, ps=ps, pt=pt, kt=kt, kn_=kn_, h=h, qn_=qn_: e.activation(
                pt[0:kn_, 0:qn_], ps[0:kn_, 0:qn_], AF.Exp, bias=negck[0:kn_, kt, h:h + 1], scale=SCALE),
                reads=[tp, t_nck], writes=[tpt])
            b.mm(psO[:, 0:qn_], Vt[0:kn_, kt, :], pt[0:kn_, 0:qn_], idx == 0, idx == len(kts) - 1, reads=[t_vt, tpt], writes=[tpO])
            b.mm(psD[:, 0:qn_], c['ones_b'][0:kn_, :], pt[0:kn_, 0:qn_], idx == 0, idx == len(kts) - 1, reads=[c['t'], tpt], writes=[tpD])
        b.op('dve', lambda e, psD=psD, qn_=qn_: e.reciprocal(REC[:, 0:qn_], psD[:, 0:qn_]), reads=[tpD], writes=[t_rec])
        ob, tob = OB[qi % 2], t_ob[qi % 2]
        b.op('dve', lambda e, psO=psO, ob=ob, qn_=qn_: e.tensor_tensor(ob[:, 0:qn_], psO[:, 0:qn_], REC[:, 0:qn_], ALU.mult),
             reads=[tpO, t_rec], writes=[tob])
        b.dma('sp', f'fox_o{qi%2}', dr['yT0'][LW + h * 128:LW + (h + 1) * 128, q0:q0 + qn_], ob[:, 0:qn_], reads=[tob])

    loads(0)
    for j in range(len(QT)):
        prologue_step(0, j)
    for h in range(FH):
        if h + 1 < FH:
            loads(h + 1)
        for qi in range(len(QT)):
            attention(h, qi)
            if h + 1 < FH:
                prologue_step(h + 1, qi)
    ph.done()


def phase_ret(nc, cfg, dr):
    ph = Phase(nc, "ret")
    b = ph.b
    L, LP, NT, RH, RQK, RV = cfg.L, cfg.LP, cfg.NT, cfg.RH, cfg.RQK, cfg.RV
    ph.psum(6)
    PSO = [(ph.ps[4], ph.t_ps[4]), (ph.ps[5], ph.t_ps[5])]
    ph.ps, ph.t_ps = ph.ps[:4], ph.t_ps[:4]
    PST = ph.st.enter_context(nc.psum_tensor("ret_pst", [128, 1024], F32))
    t_pst = b.tile("pst")
    c = load_consts(ph, dr)
    make_eps(ph, c)
    cosT = ph.sb("cosT", [128, LP], F32)
    sinT = ph.sb("sinT", [128, LP], F32)
    b.dma('sp', 'cst', cosT[:, :], dr['c_cos'], writes=[c['t']])
    b.dma('sp', 'cst', sinT[:, :], dr['c_sin'], writes=[c['t']])
    Dm = ph.sb("Dm", [128, RH, 128], F32)
    tq = ph.sb("tq", [128, RH, 128], F32)
    kd = ph.sb("kd", [128, RH], F32)
    b.dma('sp', 'cst', Dm[:, :, :], dr['c_Dm'], writes=[c['t']])
    b.dma('sp', 'cst', tq[:, :, :], dr['c_tq'], writes=[c['t']])
    b.dma('sp', 'cst', kd[:, :], dr['c_kd'], writes=[c['t']])
    rg = load_cols(ph, c, "rg", dr['c_ret_norm'], RV)

    RC = 1056
    XC = [[ph.sb(f"xc{j}{i}", [128, RC], F32) for i in range(2)] for j in range(2)]
    t_xc = [b.tiles(f"xc{j}_", 2) for j in range(2)]
    TA = ph.sb("ta", [128, RC], F32); t_ta = b.tile("ta")
    TB_ = ph.sb("tb", [128, RC], F32); t_tb = b.tile("tb")
    Qb = [ph.sb(f"qb{i}", [128, 2, LP], BF16) for i in range(2)]; t_qb = b.tiles("qb", 2)
    Kb = [ph.sb(f"kb{i}", [128, 2, LP], BF16) for i in range(2)]; t_kb = b.tiles("kb", 2)
    VF = [ph.sb(f"vf{i}", [128, 4, 384], F32) for i in range(2)]; t_vf = b.tiles("vf", 2)
    GF = [ph.sb(f"gf{i}", [128, 4, 384], F32) for i in range(2)]; t_gf = b.tiles("gf", 2)
    S = ph.sb("S", [128, 2, 512], F32); t_S = b.tile("S")
    Sb = ph.sb("Sb", [128, 2, 512], BF16); t_Sb = b.tile("Sb")
    sTb = [ph.sb(f"sTb{i}", [128, 128], BF16) for i in range(2)]; t_sTb = b.tiles("sTb", 2)
    Vb = [ph.sb(f"Vb{i}", [128, 512], BF16) for i in range(2)]; t_Vb = b.tiles("Vb", 2)
    Kdb = [ph.sb(f"Kdb{i}", [128, 256], BF16) for i in range(2)]; t_Kdb = b.tiles("Kdb", 2)
    Qdb = [ph.sb(f"Qdb{i}", [128, 2, 128], BF16) for i in range(2)]; t_Qdb = b.tiles("Qdb", 2)
    junk = ph.sb("junk", [128, 512], F32); t_junk = b.tile("junk")
    ssc = [ph.sb(f"ssc{i}", [128, 1], F32) for i in range(2)]; t_ssc = b.tiles("ssc", 2)
    on = [ph.sb(f"on{i}", [128, 512], F32) for i in range(2)]; t_on = b.tiles("on", 2)
    yb = [ph.sb(f"yb{i}", [128, 4, 128], BF16) for i in range(2)]; t_yb = b.tiles("yb", 2)
    GRP = split(LP, 384)
    NG = len(GRP)
    assert GRP[-1][1] >= 256 or NG == 1
    xcnt = [0]

    def rotary(h):
        hs = h % 2
        for (zsrc, Xb, t_xb) in ((dr['zq'], Qb[hs], t_qb[hs]), (dr['zk'], Kb[hs], t_kb[hs])):
            for (a0, an) in split(LP, RC):
                j = xcnt[0] % 2
                xcnt[0] += 1
                lo = max(a0, 112)
                for i in range(2):
                    if a0 < 112:
                        b.op('pool', lambda e, j=j, i=i, a0=a0: e.memset(XC[j][i][:, 0:112 - a0], 0.0), writes=[t_xc[j][i]])
                    b.dma('pool', f'ret_x{j}{i}', XC[j][i][:, lo - a0:an],
                          zsrc[h * 256 + i * 128:h * 256 + (i + 1) * 128, lo - 112:a0 + an - 112], writes=[t_xc[j][i]])
                X0, X1 = XC[j]
                tx0, tx1 = t_xc[j]
                b.op('pool', lambda e, a0=a0, an=an, X1=X1: e.tensor_tensor(TA[:, 0:an], X1[:, 0:an], sinT[:, a0:a0 + an], ALU.mult), reads=[tx1, c['t']], writes=[t_ta])
                b.op('pool', lambda e, a0=a0, an=an, X0=X0: e.tensor_tensor(TB_[:, 0:an], X0[:, 0:an], cosT[:, a0:a0 + an], ALU.mult), reads=[tx0, c['t']], writes=[t_tb])
                b.op('pool', lambda e, a0=a0, an=an, Xb=Xb: e.tensor_tensor(Xb[:, 0, a0:a0 + an], TB_[:, 0:an], TA[:, 0:an], ALU.subtract), reads=[t_ta, t_tb], writes=[t_xb])
                b.op('pool', lambda e, a0=a0, an=an, X0=X0: e.tensor_tensor(TA[:, 0:an], X0[:, 0:an], sinT[:, a0:a0 + an], ALU.mult), reads=[tx0, c['t']], writes=[t_ta])
                b.op('pool', lambda e, a0=a0, an=an, X1=X1: e.tensor_tensor(TB_[:, 0:an], X1[:, 0:an], cosT[:, a0:a0 + an], ALU.mult), reads=[tx1, c['t']], writes=[t_tb])
                b.op('pool', lambda e, a0=a0, an=an, Xb=Xb: e.tensor_tensor(Xb[:, 1, a0:a0 + an], TB_[:, 0:an], TA[:, 0:an], ALU.add), reads=[t_ta, t_tb], writes=[t_xb])

    def load_group(h, gi):
        g0, gn = GRP[gi]
        vi = (h * NG + gi) % 2
        vf, tvf, gf, tgf = VF[vi], t_vf[vi], GF[vi], t_gf[vi]
        lo = max(g0, 112)
        if g0 < 112:
            b.op('dve', lambda e, vf=vf: e.memset(vf[:, :, 0:112], 0.0), writes=[tvf])
            b.op('dve', lambda e, gf=gf: e.memset(gf[:, :, 0:112], 0.0), writes=[tgf])
        b.dma('sp', f'ret_v{vi}', vf[:, :, lo - g0:gn],
              dr['zv'][h * 512:(h + 1) * 512, lo - 112:g0 + gn - 112].rearrange("(i p) t -> p i t", p=128), writes=[tvf])
        b.dma('sp', f'ret_g{vi}', gf[:, :, lo - g0:gn],
              dr['zg'][h * 512:(h + 1) * 512, lo - 112:g0 + gn - 112].rearrange("(i p) t -> p i t", p=128), writes=[tgf])
        b.op('act', lambda e, gf=gf, gn=gn: e.activation(gf[:, :, 0:gn], gf[:, :, 0:gn], AF.Silu), reads=[tgf], writes=[tgf])
        b.op('dve', lambda e, gf=gf, gn=gn, h=h: e.tensor_tensor(gf[:, :, 0:gn], gf[:, :, 0:gn],
                                                           rg[:, h * 4:h * 4 + 4].unsqueeze(2).broadcast_to([128, 4, gn]), ALU.mult),
             reads=[tgf, c['t']], writes=[tgf])

    def front_a(h, r, k):
        hs = h % 2
        p = k % 2
        gi, go = divmod(r, 3)
        vi = (h * NG + gi) % 2
        vf, tvf = VF[vi], t_vf[vi]
        c0 = r * 128
        o0 = go * 128
        ps, tp = ph.bank()
        for i in range(2):
            b.mm(ps[:, 0:128], Kb[hs][:, i, c0:c0 + 128], Qb[hs][:, i, c0:c0 + 128], i == 0, i == 1, reads=[t_kb[hs], t_qb[hs]], writes=[tp])
        b.op('dve', lambda e, ps=ps, h=h, p=p: e.tensor_tensor(sTb[p][:, :], ps[:, 0:128], Dm[:, h, :], ALU.mult), reads=[tp, c['t']], writes=[t_sTb[p]])
        ps, tp = ph.bank()
        for i in range(4):
            b.op('pe', lambda e, ps=ps, i=i, vf=vf, o0=o0: e.transpose(ps[:, i * 128:(i + 1) * 128], vf[:, i, o0:o0 + 128], c['ident_f'][:, :]),
                 reads=[tvf, c['t']], writes=[tp])
        b.op('act', lambda e, ps=ps, p=p: e.activation(Vb[p][:, :], ps[:, :], AF.Copy), reads=[tp], writes=[t_Vb[p]])
        ps, tp = ph.bank()
        psb = ps[:, 0:128].bitcast(BF16)
        for i in range(2):
            b.op('pe', lambda e, psb=psb, i=i, c0=c0, hs=hs: e.transpose(psb[:, i * 128:(i + 1) * 128], Kb[hs][:, i, c0:c0 + 128], c['ident_b'][:, :]),
                 reads=[t_kb[hs], c['t']], writes=[tp])
        b.op('dve', lambda e, psb=psb, h=h, p=p: e.tensor_scalar_mul(Kdb[p][:, :], psb[:, 0:256], kd[:, h:h + 1]), reads=[tp, c['t']], writes=[t_Kdb[p]])
        b.op('dve', lambda e, c0=c0, h=h, hs=hs, p=p: e.tensor_tensor(Qdb[p][:, :, :], Qb[hs][:, :, c0:c0 + 128],
                                                                 tq[:, h, :].unsqueeze(1).broadcast_to([128, 2, 128]), ALU.mult),
             reads=[t_qb[hs], c['t']], writes=[t_Qdb[p]])

    def front_b(h, r, k):
        p = k % 2
        gam = 1.0 - 2.0 ** (-5.0 - h)
        cd128 = gam ** 128
        if r == 0:
            b.op('dve', lambda e: e.memset(S[:, :, :], 0.0), writes=[t_S])
            b.op('dve', lambda e: e.memset(Sb[:, :, :], 0.0), writes=[t_Sb])
        pso, tpo = PSO[p]
        b.mm(pso[:, :], sTb[p][:, :], Vb[p][:, :], True, False, reads=[t_sTb[p], t_Vb[p]], writes=[tpo])
        for i in range(2):
            b.mm(pso[:, :], Qdb[p][:, i, :], Sb[:, i, :], False, i == 1, reads=[t_Qdb[p], t_Sb], writes=[tpo])
        for i in range(2):
            b.mm(PST[:, i * 512:(i + 1) * 512], Kdb[p][:, i * 128:(i + 1) * 128], Vb[p][:, :], True, True, reads=[t_Kdb[p], t_Vb[p]], writes=[t_pst])
        b.op('dve', lambda e, cd128=cd128: e.scalar_tensor_tensor(S[:, :, :], S[:, :, :], cd128, PST[:, :].rearrange("p (i e) -> p i e", i=2), ALU.mult, ALU.add),
             reads=[t_pst, t_S], writes=[t_S])
        b.op('act', lambda e: e.activation(Sb[:, :, :], S[:, :, :], AF.Copy), reads=[t_S], writes=[t_Sb])
        b.op('act', lambda e, pso=pso, p=p: e.activation(junk[:, :], pso[:, :], AF.Square, accum_out=ssc[p][:, 0:1]), reads=[tpo], writes=[t_junk, t_ssc[p]])
        b.op('act', lambda e, p=p: e.activation(ssc[p][:, :], ssc[p][:, :], AF.Ln, bias=c['eps'][:, 0:1], scale=1.0 / 512), reads=[t_ssc[p], c['t']], writes=[t_ssc[p]])
        b.op('act', lambda e, p=p: e.activation(ssc[p][:, :], ssc[p][:, :], AF.Exp, scale=-0.5), reads=[t_ssc[p]], writes=[t_ssc[p]])
        b.op('act', lambda e, pso=pso, p=p: e.activation(on[p][:, :], pso[:, :], AF.Copy, scale=ssc[p][:, 0:1]), reads=[tpo, t_ssc[p]], writes=[t_on[p]])

    def back(h, r, k):
        p = k % 2
        gi, go = divmod(r, 3)
        vi = (h * NG + gi) % 2
        gf, tgf = GF[vi], t_gf[vi]
        c0 = r * 128
        o0 = go * 128
        ps, tp = ph.bank()
        for i in range(4):
            b.op('pe', lambda e, ps=ps, i=i, p=p: e.transpose(ps[:, i * 128:(i + 1) * 128], on[p][:, i * 128:(i + 1) * 128], c['ident_f'][:, :]),
                 reads=[t_on[p], c['t']], writes=[tp])
        y, ty = yb[p], t_yb[p]
        b.op('dve', lambda e, ps=ps, y=y, gf=gf, o0=o0: e.tensor_tensor(
            y[:, :, :], ps[:, :].rearrange("p (i t) -> p i t", i=4), gf[:, :, o0:o0 + 128], ALU.mult),
            reads=[tp, tgf], writes=[ty])
        lo = max(c0, 112)
        b.dma('sp', f'ret_y{p}',
              dr['yT1'][h * 512:(h + 1) * 512, lo - 112:c0 + 128 - 112].rearrange("(i p) t -> p i t", p=128),
              y[:, :, lo - c0:128], reads=[ty])

    seq = [(h, r) for h in range(RH) for r in range(NT)]
    rotary(0)
    if RH > 1:
        rotary(1)
    load_group(0, 0)
    front_a(seq[0][0], seq[0][1], 0)
    prev = None
    for k, (h, r) in enumerate(seq):
        if k + 1 < len(seq):
            front_a(seq[k + 1][0], seq[k + 1][1], k + 1)
        front_b(h, r, k)
        if prev is not None:
            back(prev[0], prev[1], k - 1)
        prev = (h, r)
        gi, go = divmod(r, 3)
        if go == 0:
            nh, ng = (h, gi + 1) if gi + 1 < NG else (h + 1, 0)
            if nh < RH:
                load_group(nh, ng)
        if r == NT - 1 and h + 2 < RH:
            rotary(h + 2)
    back(prev[0], prev[1], len(seq) - 1)
    ph.done()


def phase_final(nc, cfg, dr):
    ph = Phase(nc, "fin")
    b = ph.b
    D, L = cfg.D, cfg.L
    KC = D // 128
    ph.psum(8)
    c = load_consts(ph, dr)
    HI = [ph.sb(f"hi{i}", [128, KC, 128], F32) for i in range(2)]; t_hi = [b.tiles(f"hi{i}_", kparts(KC)) for i in range(2)]
    HO = [ph.sb(f"ho{i}", [128, D], F32) for i in range(2)]; t_ho = b.tiles("ho", 2)
    for ti, (s, n) in enumerate(split(cfg.SEQ, 128)):
        hi, thi, ho, tho = HI[ti % 2], t_hi[ti % 2], HO[ti % 2], t_ho[ti % 2]
        for pi_, (k0, kn) in enumerate(split(KC, 8)):
            b.dma('sp', f'fin_i{ti%2}_{pi_}', hi[:, k0:k0 + kn, 0:n],
                  dr['hT'][k0 * 128:(k0 + kn) * 128, NMETA + s:NMETA + s + n].rearrange("(kc p) t -> p kc t", p=128), writes=[thi[pi_]])
        for g in range(KC // 4):
            ps, tp = ph.bank()
            for q in range(4):
                kc = g * 4 + q
                b.op('pe', lambda e, ps=ps, q=q, hi=hi, kc=kc, n=n: e.transpose(ps[0:n, q * 128:(q + 1) * 128], hi[:, kc, 0:n], c['ident_f'][:, :]),
                     reads=[thi[kc // 8], c['t']], writes=[tp])
            if g % 2 == 0:
                b.op('act', lambda e, ps=ps, ho=ho, g=g, n=n: e.activation(ho[0:n, g * 512:(g + 1) * 512], ps[0:n, :], AF.Copy), reads=[tp], writes=[tho])
            else:
                b.op('dve', lambda e, ps=ps, ho=ho, g=g, n=n: e.tensor_copy(ho[0:n, g * 512:(g + 1) * 512], ps[0:n, :]), reads=[tp], writes=[tho])
        b.dma('sp', f'fin_o{ti%2}', dr['out'][s:s + n, :], ho[0:n, :], reads=[tho])
    ph.done()


def host_consts(cfg):
    L, LP, RH = cfg.L, cfg.LP, cfg.RH
    cst = {}
    cst['c_ident'] = np.eye(128, dtype=np.float32)
    p = np.arange(128)[:, None, None]
    mi = np.arange(4)[None, :, None]
    cc = np.arange(512)[None, None, :]
    cst['c_foxmask'] = np.where(cc - 128 * mi - p >= 0, 0.0, -30000.0).astype(np.float32)
    half = 128
    inv = (10000.0 ** (-np.arange(half, dtype=np.float32) / half)).astype(np.float32)
    pos = (np.arange(LP) - 112).astype(np.float32)
    ang = (pos[None, :] * inv[:, None]).astype(np.float32)
    cst['c_cos'] = np.cos(ang).astype(np.float32)
    cst['c_sin'] = np.sin(ang).astype(np.float32)
    lg = np.log(1.0 - 2.0 ** (-5.0 - np.arange(RH, dtype=np.float64)))
    m = np.arange(128)[:, None]
    cq = np.arange(128)[None, :]
    same = (m // 64) == (cq // 64)
    earlier = (m // 64) < (cq // 64)
    Dm = np.zeros((128, RH, 128), np.float64)
    for h in range(RH):
        d = np.where(same, np.exp(lg[h] * np.abs(cq - m)), np.where(earlier, np.exp(lg[h] * (cq - m)), 0.0))
        Dm[:, h, :] = d / 16.0
    cst['c_Dm'] = Dm.astype(np.float32)
    tq = np.exp(lg[None, :, None] * (np.arange(128)[None, None, :] + 1.0)) * np.ones((128, 1, 1))
    cst['c_tq'] = tq.astype(np.float32)
    kd = np.exp(lg[None, :] * (127.0 - np.arange(128)[:, None])) / 16.0
    cst['c_kd'] = kd.astype(np.float32)
    return cst


PARAMS = ["meta_tokens", "ab_norm", "ab_w_in", "ab_b_f", "ab_conv_w", "ab_conv_b", "ab_w_a", "ab_b_a", "ab_w_x",
          "ab_b_x", "ab_lambda", "ab_q_norm", "ab_k_norm", "ab_w_out", "c_norm", "c_w_in", "c_ret_norm", "c_w_out",
          "ffn_norm", "ffn_w_gate", "ffn_w_up", "ffn_w_down"]


def build_program(cfg, shapes, upto=99):
    nc = bass.Bass("TRN2", target_bir_lowering=False)
    dr = {}
    dr['x'] = nc.dram_tensor("x", [cfg.SEQ, cfg.D], F32, kind="ExternalInput").ap()
    for k in PARAMS:
        shp = [s for s in shapes[k]]
        if shp[0] == 1 and k != "ffn_norm" and not k.startswith("ffn_w"):
            shp = shp[1:]
        nm = 'meta' if k == 'meta_tokens' else k
        dr[nm] = nc.dram_tensor(nm, shp, F32, kind="ExternalInput").ap()
    cst = host_consts(cfg)
    for k, v in cst.items():
        dr[k] = nc.dram_tensor(k, list(v.shape), F32, kind="ExternalInput").ap()
    dr['out'] = nc.dram_tensor("out", [cfg.SEQ, cfg.D], F32, kind="ExternalOutput").ap()
    D, L = cfg.D, cfg.L
    dr['hT'] = nc.dram_tensor("hT", [D, L], F32, kind="Internal").ap()
    dr['zT0'] = nc.dram_tensor("zT0", [cfg.AB_IN, L], F32, kind="Internal").ap()
    dr['yT0'] = nc.dram_tensor("yT0", [cfg.AB_MIX, L], BF16, kind="Internal").ap()
    dr['HT'] = nc.dram_tensor("HT", [cfg.DFF, L], BF16, kind="Internal").ap()
    dr['zq'] = nc.dram_tensor("zq", [cfg.RQK, L], F32, kind="Internal").ap()
    dr['zk'] = nc.dram_tensor("zk", [cfg.RQK, L], F32, kind="Internal").ap()
    dr['zv'] = nc.dram_tensor("zv", [cfg.RV, L], F32, kind="Internal").ap()
    dr['zg'] = nc.dram_tensor("zg", [cfg.RV, L], F32, kind="Internal").ap()
    dr['yT1'] = nc.dram_tensor("yT1", [cfg.RV, L], BF16, kind="Internal").ap()
    dr['cqD'] = nc.dram_tensor("cqD", [cfg.FH, 3, L], BF16, kind="Internal").ap()

    steps = [
        lambda: phase_prep(nc, cfg, dr),
        lambda: phase_inproj(nc, cfg, dr, "in0", dr['ab_norm'], dr['ab_w_in'], cfg.AB_IN, [(0, cfg.AB_IN, dr['zT0'])]),
        lambda: phase_lru(nc, cfg, dr),
        lambda: phase_fox(nc, cfg, dr),
        lambda: phase_outproj(nc, cfg, dr, "op0", dr['yT0'], cfg.AB_MIX, dr['ab_w_out'], 1040, 512),
        lambda: phase_ffn_gu(nc, cfg, dr, "gu0", dr['ffn_norm'][0], dr['ffn_w_gate'][0], dr['ffn_w_up'][0]),
        lambda: phase_outproj(nc, cfg, dr, "dn0", dr['HT'], cfg.DFF, dr['ffn_w_down'][0], 528, 128),
        lambda: phase_inproj(nc, cfg, dr, "in1", dr['c_norm'], dr['c_w_in'], cfg.RET_IN,
                             [(0, cfg.RQK, dr['zq']), (cfg.RQK, 2 * cfg.RQK, dr['zk']),
                              (2 * cfg.RQK, 2 * cfg.RQK + cfg.RV, dr['zv']), (2 * cfg.RQK + cfg.RV, cfg.RET_IN, dr['zg'])]),
        lambda: phase_ret(nc, cfg, dr),
        lambda: phase_outproj(nc, cfg, dr, "op1", dr['yT1'], cfg.RV, dr['c_w_out'], 528, 256),
        lambda: phase_ffn_gu(nc, cfg, dr, "gu1", dr['ffn_norm'][1], dr['ffn_w_gate'][1], dr['ffn_w_up'][1]),
        lambda: phase_outproj(nc, cfg, dr, "dn1", dr['HT'], cfg.DFF, dr['ffn_w_down'][1], 528, 128),
    ]
    for i, s in enumerate(steps):
        if i < upto:
            s()
    phase_final(nc, cfg, dr)
    return nc, cst


_CACHE = {}


def kernel(**inputs):
    x = np.asarray(inputs["x"], dtype=np.float32)
    B, SEQ, D = x.shape
    cfg = Cfg(D, SEQ)
    shapes = {k: tuple(np.asarray(inputs[k]).shape) for k in PARAMS}
    key = (D, SEQ)
    if key not in _CACHE:
        _CACHE[key] = build_program(cfg, shapes)
    nc, cst = _CACHE[key]
    shared = {}
    for k in PARAMS:
        a = np.ascontiguousarray(np.asarray(inputs[k], dtype=np.float32))
        if a.shape[0] == 1 and k != "ffn_norm" and not k.startswith("ffn_w"):
            a = a[0]
        shared['meta' if k == 'meta_tokens' else k] = a
    shared.update(cst)
    ncores = 8
    hot = list(range(B))
    in_maps = []
    for cidx in range(ncores):
        m = dict(shared)
        m['x'] = np.ascontiguousarray(x[cidx % B])
        in_maps.append(m)
    res = run_bass_kernel_spmd(nc, in_maps, core_ids=list(range(ncores)))
    out = np.stack([np.asarray(res.results[hot[bi]]["out"], dtype=np.float32) for bi in range(B)], axis=0)
    return out
```
